# Optimizing a Trainium2 kernel written in Bass

```python
import math
import jax, jax.numpy as jnp
from jax import lax
import numpy as np

D_MODEL = 1024
BATCH = 32
SEQ = 256
DEPTH = 4
DEC_BATCH = 2
DEC_SEQ = 4096
PAST_LEN = 256

GRID_W = 64
N_RET_HEADS = 4
RET_DK = 128
RET_DV = 128
RET_W = N_RET_HEADS * RET_DV
RET_CHUNK = 128
CONV_W = 512
CONV_K = 3
N_NA_HEADS = 8
NA_HEAD_DIM = 64
NA_W = N_NA_HEADS * NA_HEAD_DIM
NA_KH = 8
NA_KW = 16
BRANCH_W = 512
N_BRANCH = 3
N_SPLIT = 10
MIX_IN = N_SPLIT * BRANCH_W
D_FF = 2816
N_MOD = 9
ROPE_BASE = 10000.0
EPS = 1e-6
NEG_INF = -1e30

kernel_name = 'hybrid_diffusion_retention_conv_natten_step'


def rmsnorm(x, g):
    x32 = x.astype(jnp.float32)
    y = x32 * lax.rsqrt(jnp.mean(x32 * x32, axis=-1, keepdims=True) + EPS)
    return (y * g.astype(jnp.float32)).astype(x.dtype)


def split_heads(a, n_heads):
    b, l, _ = a.shape
    return a.reshape(b, l, n_heads, -1).transpose(0, 2, 1, 3)


def merge_heads(a):
    b, h, l, d = a.shape
    return a.transpose(0, 2, 1, 3).reshape(b, l, h * d)


def swiglu(u, w1, w2):
    a, b = jnp.split(u @ w1, 2, axis=-1)
    return (jax.nn.silu(a) * b) @ w2


def axial_rope(x):
    l = x.shape[2]
    t = jnp.arange(l)
    half = x.shape[-1] // 2
    quarter = half // 2
    inv_freq = ROPE_BASE ** (-jnp.arange(quarter, dtype=jnp.float32) * 2.0 / half)

    def rotate(xa, pos):
        ang = pos.astype(jnp.float32)[:, None] * inv_freq[None, :]
        cos, sin = jnp.cos(ang), jnp.sin(ang)
        x1 = xa[..., :quarter].astype(jnp.float32)
        x2 = xa[..., quarter:].astype(jnp.float32)
        return jnp.concatenate([x1 * cos - x2 * sin, x1 * sin + x2 * cos], axis=-1)

    out = jnp.concatenate([rotate(x[..., :half], t // GRID_W),
                           rotate(x[..., half:], t % GRID_W)], axis=-1)
    return out.astype(x.dtype)


def retention_scan(q, k, v, log_g, s0):
    b, h, l, _ = q.shape
    c = RET_CHUNK
    n = l // c
    idx = jnp.arange(c, dtype=jnp.float32)
    diff = idx[:, None] - idx[None, :]
    decay_in = jnp.where(diff >= 0, jnp.exp(log_g[:, None, None] * jnp.maximum(diff, 0.0)), 0.0)
    xi = jnp.exp(log_g[:, None] * (idx + 1.0))[None, :, :, None]
    zeta = jnp.exp(log_g[:, None] * (c - 1.0 - idx))[None, :, :, None]
    g_chunk = jnp.exp(log_g * c)[None, :, None, None]

    def to_chunks(a):
        return jnp.moveaxis(a.reshape(b, h, n, c, a.shape[-1]), 2, 0)

    def step(s, inp):
        qi, ki, vi = inp
        inner = jnp.einsum('bhqd,bhkd->bhqk', qi, ki) * decay_in[None]
        o = (jnp.einsum('bhqk,bhkv->bhqv', inner, vi)
             + jnp.einsum('bhqd,bhdv->bhqv', qi, s) * xi)
        s_new = s * g_chunk + jnp.einsum('bhkd,bhkv->bhdv', ki * zeta, vi)
        return s_new, o

    s_fin, o = lax.scan(step, s0.astype(jnp.float32), (to_chunks(q), to_chunks(k), to_chunks(v)))
    o = jnp.moveaxis(o, 0, 2).reshape(b, h, l, -1)
    return o, s_fin


def head_norm(o):
    mu = jnp.mean(o, axis=-1, keepdims=True)
    var = jnp.mean(jnp.square(o - mu), axis=-1, keepdims=True)
    return (o - mu) * lax.rsqrt(var + EPS)


def short_conv(b_gate, c_gate, h, w, bias):
    z = c_gate * h
    zp = jnp.pad(z, ((0, 0), (1, 1), (0, 0)))
    y = zp[:, :-2] * w[0] + zp[:, 1:-1] * w[1] + zp[:, 2:] * w[2] + bias
    return b_gate * y


def context_attention(q, k, v):
    s = jnp.einsum('bhqd,bhkd->bhqk', q, k).astype(jnp.float32) * (NA_HEAD_DIM ** -0.5)
    p = jax.nn.softmax(s, axis=-1).astype(v.dtype)
    return jnp.einsum('bhqk,bhkd->bhqd', p, v)


def neighbourhood_attention(q, k, v, k_ctx, v_ctx, rpb):
    b, h, l, hd = q.shape
    rows = l // GRID_W
    kh = min(NA_KH, rows)
    r = jnp.arange(rows)
    r0 = jnp.clip(r - NA_KH // 2, 0, rows - kh)
    row_idx = r0[:, None] + jnp.arange(kh)[None, :]
    cq = jnp.arange(GRID_W)
    c0 = jnp.clip(cq - NA_KW // 2, 0, GRID_W - NA_KW)
    col_ok = (cq[None, :] >= c0[:, None]) & (cq[None, :] < c0[:, None] + NA_KW)
    dr = row_idx - r[:, None] + (NA_KH - 1)
    dc = jnp.clip(cq[None, :] - cq[:, None] + (NA_KW - 1), 0, 2 * NA_KW - 2)
    bias = rpb.astype(jnp.float32)[:, dr[:, :, None, None], dc[None, None, :, :]]
    bias = jnp.where(col_ok[None, None, None], bias, NEG_INF)
    bias = bias.transpose(0, 1, 3, 2, 4).reshape(h, rows, GRID_W, kh * GRID_W)
    qg = q.reshape(b, h, rows, GRID_W, hd)
    kg = k.reshape(b, h, rows, GRID_W, hd)[:, :, row_idx].reshape(b, h, rows, kh * GRID_W, hd)
    vg = v.reshape(b, h, rows, GRID_W, hd)[:, :, row_idx].reshape(b, h, rows, kh * GRID_W, hd)
    scale = hd ** -0.5
    s_loc = jnp.einsum('bhrqd,bhrkd->bhrqk', qg, kg).astype(jnp.float32) * scale + bias[None]
    s_ctx = jnp.einsum('bhrqd,bhkd->bhrqk', qg, k_ctx).astype(jnp.float32) * scale
    p = jax.nn.softmax(jnp.concatenate([s_loc, s_ctx], axis=-1), axis=-1).astype(v.dtype)
    n_loc = kh * GRID_W
    o = (jnp.einsum('bhrqk,bhrkd->bhrqd', p[..., :n_loc], vg)
         + jnp.einsum('bhrqk,bhkd->bhrqd', p[..., n_loc:], v_ctx))
    return o.reshape(b, h, l, hd)


def token_mixer(u, l, W, ctx):
    b, n, _ = u.shape
    proj = u @ W['w_in'][l]
    q_r, k_r, v_r, g_r, c_b, c_c, c_h, q_n, k_n, v_n = jnp.split(proj, N_SPLIT, axis=-1)
    q_r = split_heads(q_r, N_RET_HEADS) * (RET_DK ** -0.5)
    k_r = split_heads(k_r, N_RET_HEADS)
    v_r = split_heads(v_r, N_RET_HEADS)
    log_g = jax.nn.log_sigmoid(W['ret_decay_logit'][l].astype(jnp.float32))
    if ctx is None:
        s_f0 = jnp.zeros((b, N_RET_HEADS, RET_DK, RET_DV), jnp.float32)
        s_b0 = s_f0
    else:
        s_f0, s_b0, k_ctx, v_ctx = ctx
        q_r = axial_rope(q_r)
        k_r = axial_rope(k_r)
    o_f, s_f = retention_scan(q_r, k_r, v_r, log_g[0], s_f0)
    o_b, s_b = retention_scan(jnp.flip(q_r, 2), jnp.flip(k_r, 2), jnp.flip(v_r, 2), log_g[1], s_b0)
    o_ret = head_norm(o_f + jnp.flip(o_b, 2))
    ret_out = merge_heads(o_ret).astype(u.dtype) * jax.nn.silu(g_r)
    conv_out = short_conv(c_b, c_c, c_h, W['conv_w'][l], W['conv_b'][l])
    q_n = split_heads(q_n, N_NA_HEADS)
    k_n = split_heads(k_n, N_NA_HEADS)
    v_n = split_heads(v_n, N_NA_HEADS)
    if ctx is None:
        o_na = context_attention(q_n, k_n, v_n)
        ctx_tensors = (jnp.stack([s_f, s_b], axis=1), k_n, v_n)
    else:
        o_na = neighbourhood_attention(q_n, k_n, v_n, k_ctx, v_ctx, W['na_rpb'][l])
        ctx_tensors = None
    na_out = merge_heads(o_na)
    branches = jnp.stack([ret_out, conv_out, na_out], axis=2)
    widened = jnp.einsum('blnc,ncd->blnd', branches, W['w_branch'][l])
    gates = jax.nn.sigmoid(u @ W['w_merge'][l] + W['b_merge'][l]).reshape(b, n, N_BRANCH, D_MODEL)
    merged = jnp.sum(gates * widened, axis=2)
    return merged @ W['w_out'][l], ctx_tensors


def trunk_layer(x, cond, l, W, ctx):
    bm = cond.shape[0]
    mod = (jax.nn.silu(cond) @ W['w_mod'][l] + W['b_mod'][l]).reshape(bm, N_MOD, D_MODEL)
    g = W['norm_g'][l]
    h = rmsnorm(x, g[0]) * (1.0 + mod[:, None, 1]) + mod[:, None, 0]
    x = x + 0.5 * mod[:, None, 2] * swiglu(h, W['ffn_w1'][l, 0], W['ffn_w2'][l, 0])
    u = rmsnorm(x, g[1]) * (1.0 + mod[:, None, 4]) + mod[:, None, 3]
    mix, ctx_tensors = token_mixer(u, l, W, ctx)
    x = x + mod[:, None, 5] * mix
    h = rmsnorm(x, g[2]) * (1.0 + mod[:, None, 7]) + mod[:, None, 6]
    x = x + 0.5 * mod[:, None, 8] * swiglu(h, W['ffn_w1'][l, 1], W['ffn_w2'][l, 1])
    return x, ctx_tensors


def setup_inputs(seed: int = 0) -> dict:
    key = jax.random.key(seed)
    ks = jax.random.split(key, 24)
    f32 = jnp.float32

    def nrm(k, shape, s):
        return jax.random.normal(k, shape, f32) * s

    a = 5.0 + jnp.arange(N_RET_HEADS, dtype=f32)
    decay0 = jnp.log(2.0 ** a - 1.0)
    return {
        'x_prompt': nrm(ks[0], (BATCH, SEQ, D_MODEL), 1.0),
        'x_sample': nrm(ks[1], (DEC_BATCH, DEC_SEQ, D_MODEL), 1.0),
        'c': nrm(ks[2], (DEC_BATCH, D_MODEL), 1.0),
        'state_ret': nrm(ks[3], (DEC_BATCH, DEPTH, 2, N_RET_HEADS, RET_DK, RET_DV), 0.5),
        'cache_na_k': nrm(ks[4], (DEC_BATCH, DEPTH, N_NA_HEADS, PAST_LEN, NA_HEAD_DIM), 1.0),
        'cache_na_v': nrm(ks[5], (DEC_BATCH, DEPTH, N_NA_HEADS, PAST_LEN, NA_HEAD_DIM), 1.0),
        'c_ctx': nrm(ks[6], (D_MODEL,), 1.0),
        'norm_g': 1.0 + nrm(ks[7], (DEPTH, 3, D_MODEL), 0.02),
        'w_mod': nrm(ks[8], (DEPTH, D_MODEL, N_MOD * D_MODEL), 0.5 * D_MODEL ** -0.5),
        'b_mod': nrm(ks[9], (DEPTH, N_MOD * D_MODEL), 0.01),
        'ffn_w1': nrm(ks[10], (DEPTH, 2, D_MODEL, 2 * D_FF), D_MODEL ** -0.5),
        'ffn_w2': nrm(ks[11], (DEPTH, 2, D_FF, D_MODEL), D_FF ** -0.5),
        'w_in': nrm(ks[12], (DEPTH, D_MODEL, MIX_IN), D_MODEL ** -0.5),
        'ret_decay_logit': decay0 + nrm(ks[13], (DEPTH, 2, N_RET_HEADS), 0.1),
        'conv_w': nrm(ks[14], (DEPTH, CONV_K, CONV_W), CONV_K ** -0.5),
        'conv_b': nrm(ks[15], (DEPTH, CONV_W), 0.01),
        'na_rpb': nrm(ks[16], (DEPTH, N_NA_HEADS, 2 * NA_KH - 1, 2 * NA_KW - 1), 0.1),
        'w_branch': nrm(ks[17], (DEPTH, N_BRANCH, BRANCH_W, D_MODEL), BRANCH_W ** -0.5),
        'w_merge': nrm(ks[18], (DEPTH, D_MODEL, N_BRANCH * D_MODEL), D_MODEL ** -0.5),
        'b_merge': nrm(ks[19], (DEPTH, N_BRANCH * D_MODEL), 0.01),
        'w_out': nrm(ks[20], (DEPTH, D_MODEL, D_MODEL), D_MODEL ** -0.5),
        'final_g': 1.0 + nrm(ks[21], (D_MODEL,), 0.02),
    }


def reference(x_prompt, x_sample, c, state_ret, cache_na_k, cache_na_v, c_ctx,
              norm_g, w_mod, b_mod, ffn_w1, ffn_w2, w_in, ret_decay_logit, conv_w, conv_b,
              na_rpb, w_branch, w_merge, b_merge, w_out, final_g):
    W = {'norm_g': norm_g, 'w_mod': w_mod, 'b_mod': b_mod, 'ffn_w1': ffn_w1, 'ffn_w2': ffn_w2,
         'w_in': w_in, 'ret_decay_logit': ret_decay_logit, 'conv_w': conv_w, 'conv_b': conv_b,
         'na_rpb': na_rpb, 'w_branch': w_branch, 'w_merge': w_merge, 'b_merge': b_merge,
         'w_out': w_out}
    h = x_prompt
    cond_ctx = c_ctx[None, :]
    s_list, k_list, v_list = [], [], []
    for l in range(DEPTH):
        h, (s_l, k_l, v_l) = trunk_layer(h, cond_ctx, l, W, None)
        s_list.append(s_l)
        k_list.append(k_l)
        v_list.append(v_l)
    y_prompt = rmsnorm(h, final_g)
    new_state_ret = jnp.stack(s_list, axis=1)
    new_cache_na_k = jnp.stack(k_list, axis=1)
    new_cache_na_v = jnp.stack(v_list, axis=1)
    z = x_sample
    for l in range(DEPTH):
        ctx = (state_ret[:, l, 0], state_ret[:, l, 1], cache_na_k[:, l], cache_na_v[:, l])
        z, _ = trunk_layer(z, c, l, W, ctx)
    y_sample = rmsnorm(z, final_g)
    return (y_prompt, y_sample, new_state_ret, new_cache_na_k, new_cache_na_v)
```

```python
import contextlib
import os
import numpy as np
import concourse.bass as bass
import concourse.mybir as mybir
from concourse.bass_utils import run_bass_kernel_spmd

F32 = mybir.dt.float32
BF16 = mybir.dt.bfloat16
AF = mybir.ActivationFunctionType
ALU = mybir.AluOpType

D = 1024
KC = 8
NT = 1024
L = 4
DFF = 2816
JF = 22
EPS = 1e-6
NEG = -1e30

STAGE = os.environ.get("KSTAGE", "full")
C_ID, C_SW, C_RELF, C_MF, C_RELB, C_MB, C_IO1, C_IOB, C_COL = 0, 128, 256, 384, 512, 640, 768, 896, 1024
NCT = 1024 + 110
CW = 5128


class Tile:
    __slots__ = ("w", "r", "ex")

    def __init__(self):
        self.w = None
        self.r = {}
        self.ex = False


class Eng:
    def __init__(self, name, sem):
        self.name = name
        self.sem = sem
        self.cnt = 0
        self.waited = {}
        self.ops = []


class Buf:
    def __init__(self, t, ex=False):
        self.t = t
        self.tiles = {}
        self.ex = ex

    def T(self, *key):
        tl = self.tiles.get(key)
        if tl is None:
            tl = self.tiles[key] = Tile()
            tl.ex = self.ex
        return tl

    def __getitem__(self, idx):
        return self.t[idx]


class KB:
    def __init__(self, nc, stack):
        self.nc = nc
        self.stack = stack
        self.semc = 0
        self.E = {}
        for n in ("pe", "act", "dve", "pool", "sp"):
            self.E[n] = Eng(n, self.newsem("e_" + n))
        self.dsem = {}
        for q in ("sp", "act", "pool"):
            self.dsem[q] = [[self.newsem("d_%s%d" % (q, i)), 0] for i in range(6)]
        self.drr = {"sp": 0, "act": 0, "pool": 0}
        self.allsigs = {}
        self.nbank = 0
        self.zero_bias = None
        self.ccsem = None
        self.ccn = 0

    def newsem(self, name):
        self.semc += 1
        return self.stack.enter_context(self.nc.semaphore(name))

    def sbuf(self, name, shape, dt):
        return Buf(self.stack.enter_context(self.nc.sbuf_tensor(name, shape, dt)))

    def psum(self, name, shape, dt):
        return Buf(self.stack.enter_context(self.nc.psum_tensor(name, shape, dt)), ex=True)

    def _deps(self, R, W):
        deps = {}
        exr = [t for t in R if t.ex]
        if exr:
            W = list(W) + exr

        def add(sig):
            if sig is None:
                return
            s, v = sig
            k = id(s)
            if k not in deps or deps[k][1] < v:
                deps[k] = (s, v)

        for t in R:
            add(t.w)
        for t in W:
            add(t.w)
            for sig in t.r.values():
                add(sig)
        return deps

    def _mark(self, sig, R, W):
        k = id(sig[0])
        exr = [t for t in R if t.ex]
        if exr:
            W = list(W) + exr
            R = [t for t in R if not t.ex]
        for t in R:
            t.r[k] = sig
        for t in W:
            t.w = sig
            t.r = {}
        self.allsigs[k] = sig

    def emit(self, en, fn, R=(), W=()):
        e = self.E[en]
        deps = self._deps(R, W)
        waits = []
        for k, (s, v) in deps.items():
            if e.waited.get(k, 0) >= v:
                continue
            if en == "pe" and s is e.sem:
                continue
            e.waited[k] = v
            waits.append((s, v))
        e.cnt += 1
        sig = (e.sem, e.cnt)
        e.waited[id(e.sem)] = max(e.waited.get(id(e.sem), 0), 0)
        e.ops.append((waits, fn, e.sem, 1))
        self._mark(sig, R, W)

    def dma(self, q, out, in_, R=(), W=(), **kw):
        e = self.E[q]
        slot = self.dsem[q][self.drr[q] % len(self.dsem[q])]
        self.drr[q] += 1
        deps = self._deps(R, W)
        waits = []
        if slot[1] > 0:
            deps[id(slot[0])] = (slot[0], slot[1])
        for k, (s, v) in deps.items():
            if e.waited.get(k, 0) >= v:
                continue
            e.waited[k] = v
            waits.append((s, v))
        slot[1] += 16
        sig = (slot[0], slot[1])

        def fn(h, out=out, in_=in_, kw=kw):
            return h.dma_start(out=out, in_=in_, **kw)

        e.ops.append((waits, fn, slot[0], 16))
        self._mark(sig, R, W)

    def collective(self, src, dst, groups, R, W, flag_ap, flag_tile):
        e = self.E["pool"]
        csem = self.newsem("ccsem%d" % self.ccn)
        self.ccn += 1
        deps = self._deps(R, W)
        waits = []
        for k, (s_, v) in deps.items():
            if e.waited.get(k, 0) >= v:
                continue
            e.waited[k] = v
            waits.append((s_, v))

        def fn(h):
            return h.collective_compute("AllGather", ALU.bypass, replica_groups=groups, ins=[src], outs=[dst])

        e.ops.append((waits, fn, csem, 1))
        e.cnt += 1
        sig = (e.sem, e.cnt)
        e.ops.append(([(csem, 1)], lambda h: h.memset(flag_ap, 0.0), e.sem, 1))
        self._mark(sig, R, list(W) + [flag_tile])

    def finish(self):
        e = self.E["sp"]
        waits = []
        for k, (s, v) in self.allsigs.items():
            waits.append((s, v))
        e.ops.append((waits, None, None, 0))

    def replay(self, block):
        nc = self.nc
        names = {"pe": "tensor", "act": "scalar", "dve": "vector", "pool": "gpsimd", "sp": "sync"}
        for en, bn in names.items():
            ops = self.E[en].ops

            def body(h, ops=ops):
                for waits, fn, sem, inc in ops:
                    for s, v in waits:
                        h.wait_ge(s, v)
                    if fn is not None:
                        ins = fn(h)
                        ins.then_inc(sem, inc)

            getattr(block, bn)(body)

    def mm(self, out, lhsT, rhs, start, stop, R, W):
        self.emit("pe", lambda h: h.matmul(out, lhsT, rhs, start=start, stop=stop), R, W)

    def tr(self, out, in_, ident, R, W):
        self.emit("pe", lambda h: h.transpose(out, in_, ident), R, W)

    def act(self, out, in_, func, R, W, bias=None, scale=None, accum_out=None, eng="act"):
        kw = {}
        if bias is not None:
            kw["bias"] = bias
        if scale is not None:
            kw["scale"] = scale
        if accum_out is not None:
            kw["accum_out"] = accum_out
        if scale is not None and not isinstance(scale, float) and bias is None and self.zero_bias is not None:
            kw["bias"] = self.zero_bias[0]
            R = list(R) + [self.zero_bias[1]]
        self.emit(eng, lambda h: h.activation(out, in_, func, **kw), R, W)

    def tt(self, out, in0, in1, op, R, W, eng="dve"):
        self.emit(eng, lambda h: h.tensor_tensor(out, in0, in1, op), R, W)

    def ts(self, out, in0, s1, s2, op0, op1, R, W, eng="dve"):
        if op1 is None:
            self.emit(eng, lambda h: h.tensor_scalar(out, in0, s1, None, op0), R, W)
        else:
            self.emit(eng, lambda h: h.tensor_scalar(out, in0, s1, s2, op0, op1), R, W)

    def stt(self, out, in0, scalar, in1, op0, op1, R, W):
        self.emit("dve", lambda h: h.scalar_tensor_tensor(out, in0, scalar, in1, op0, op1), R, W)

    def cp(self, out, in_, R, W, eng="dve"):
        if eng == "act":
            self.emit(eng, lambda h: h.activation(out, in_, AF.Copy), R, W)
        else:
            self.emit(eng, lambda h: h.tensor_copy(out, in_), R, W)

    def recip(self, out, in_, R, W):
        self.emit("dve", lambda h: h.reciprocal(out, in_), R, W)

    def memset(self, out, val, R, W, eng="dve"):
        self.emit(eng, lambda h: h.memset(out, val), R, W)


def build_program():
    nc = bass.Bass("TRN2", target_bir_lowering=False)
    stack = contextlib.ExitStack()
    with stack:
        K = KB(nc, stack)

        def din(name, shape):
            return Buf(nc.dram_tensor(name, list(shape), F32, kind="ExternalInput").ap())

        def dout(name, shape):
            return Buf(nc.dram_tensor(name, list(shape), F32, kind="ExternalOutput").ap())

        xp = din("xp", [NT, D])
        xs = din("xs", [NT, D])
        vecg = din("vecg", [24, 128])
        ident_d = din("ident", [128, 128])
        norm_g = din("norm_g", [L, 24, 128])
        w_mod = din("w_mod", [L, D, 9 * D])
        b_mod = din("b_mod", [L, 72, 128])
        ffn_w1 = din("ffn_w1", [L, 2, D, 2 * DFF])
        ffn_w2 = din("ffn_w2", [L, 2, DFF, D])
        yp = dout("yp", [NT, D])
        ys = dout("ys", [NT, D])

        xT = K.sbuf("xT", [128, KC, 2 * NT], F32)
        hT = K.sbuf("hT", [128, KC, NT], BF16)
        ar = K.sbuf("ar", [128, 12288], F32)
        arb = ar.t.bitcast(BF16)

        class AV:
            def __init__(self, off, n, fp32=False):
                self.off, self.n, self.fp32 = off, n, fp32

            def ap(self, lo=0, hi=None):
                hi = self.n if hi is None else hi
                if self.fp32:
                    return ar.t[:, self.off // 2 + lo:self.off // 2 + hi]
                return arb[:, self.off + lo:self.off + hi]

            def T(self, lo=0, hi=None):
                hi = self.n if hi is None else hi
                m = 2 if self.fp32 else 1
                a, b_ = self.off + lo * m, self.off + hi * m
                return [ar.T(k) for k in range(a // 1024, (b_ - 1) // 1024 + 1)]

        class _Hid:
            def __getitem__(self, idx):
                return arb[:, 0:JF * NT].rearrange("p (j t) -> p j t", j=JF)[idx]

            def T(self, j, t):
                return ar.T(j)
        hid = _Hid()
        NWB = 2
        WSL = 5632
        wb = K.sbuf("wb", [128, NWB, WSL], BF16)
        wm = K.sbuf("wm", [128, 2, KC, 128], F32)
        sq = K.sbuf("sq", [128, 2, 512], BF16)
        rstd = K.sbuf("rstd", [128, 1, 512], F32)
        tmpf = K.sbuf("tmpf", [128, 3, 512], F32)
        tmpb = K.sbuf("tmpb", [128, 2, 512], BF16)

        class _Stg:
            def __getitem__(self, idx):
                return wm.t[:].rearrange("p s k c -> p s (k c)")[idx]

            def T(self, sl):
                return wm.T(sl)
        stg = _Stg()
        vst = K.sbuf("vst", [128, 128], F32)
        vecA = K.sbuf("vecA", [128, 96], F32)
        vecG = K.sbuf("vecG", [128, 24], F32)
        scond = K.sbuf("scond", [128, KC, 2], F32)
        modv = K.sbuf("modv", [128, 72, 2], F32)
        gsv = K.sbuf("gsv", [128, 3, KC, 2], F32)
        ghv = K.sbuf("ghv", [128, 3, KC, 2], F32)
        identf = K.sbuf("identf", [128, 128], F32)
        identb = K.sbuf("identb", [128, 128], BF16)
        onesb = K.sbuf("onesb", [128, 128], BF16)
        epsb = K.sbuf("epsb", [128, 1], F32)
        ps = K.psum("ps", [128, 8, 512], F32)

        reserved = set()

        def bank():
            while True:
                b = K.nbank % 8
                K.nbank += 1
                if b not in reserved:
                    return b

        K.dma("sp", identf[:], ident_d[:, :], R=[], W=[identf.T()])
        K.cp(identb[:], identf[:], R=[identf.T()], W=[identb.T()])
        K.memset(onesb[:], 1.0, R=[], W=[onesb.T()])
        K.memset(epsb[:], EPS, R=[], W=[epsb.T()])

        ld = 0
        for g, src in ((0, xp), (1, xs)):
            for tb in range(8):
                sl = ld % 2
                ld += 1
                K.dma("sp", stg[:, sl, :], src[tb * 128:(tb + 1) * 128, :], R=[], W=[stg.T(sl)])
                for hf in range(2):
                    b = bank()
                    for q in range(4):
                        kc = hf * 4 + q
                        K.tr(ps[:, b, q * 128:(q + 1) * 128], stg[:, sl, kc * 128:(kc + 1) * 128], identf[:],
                             R=[stg.T(sl), identf.T()], W=[ps.T(b)])
                    tok0 = g * NT + tb * 128
                    K.cp(xT[:, hf * 4:hf * 4 + 4, tok0:tok0 + 128],
                         ps[:, b, :].rearrange("p (q t) -> p q t", q=4),
                         R=[ps.T(b)], W=[xT.T(g, hf * 4 + q, tb // 4) for q in range(4)],
                         eng="act" if hf else "dve")

        K.dma("sp", vst[0:24, :], vecg[:, :], R=[], W=[vst.T()])
        b = bank()
        K.tr(ps[:, b, 0:24], vst[0:24, :], identf[0:24, 0:24], R=[vst.T(), identf.T()], W=[ps.T(b)])
        K.cp(vecG[:], ps[:, b, 0:24], R=[ps.T(b)], W=[vecG.T()])
        K.act(scond[:].rearrange("p k c -> p c k"), vecG[:, 0:16].rearrange("p (c k) -> p c k", c=2), AF.Silu,
              R=[vecG.T()], W=[scond.T()])

        wslot = [0]

        def load_w(dst_views_and_src, tilekey_extra=None):
            s = wslot[0] % NWB
            wslot[0] += 1
            for o, i in dst_views_and_src(s):
                K.dma("pool", o, i, R=[], W=[wb.T(s)])
            return s

        def layer_vectors(l):
            K.dma("sp", vst[0:72, :], b_mod[l], R=[], W=[vst.T()])
            K.dma("sp", vst[72:96, :], norm_g[l], R=[], W=[vst.T()])
            b = bank()
            K.tr(ps[:, b, 0:96], vst[0:96, :], identf[0:96, 0:96], R=[vst.T(), identf.T()], W=[ps.T(b)])
            K.cp(vecA[:], ps[:, b, 0:96], R=[ps.T(b)], W=[vecA.T()])
            wv = w_mod[l].rearrange("(kc p) n -> p kc n", p=128)
            for pc in range(72):
                s = pc % 2
                K.dma("sp", wm[:, s], wv[:, :, pc * 128:(pc + 1) * 128], R=[], W=[wm.T(s)])
                if pc % 8 == 0:
                    b = bank()
                c0 = (pc % 8) * 2
                for kc in range(KC):
                    K.mm(ps[:, b, c0:c0 + 2], wm[:, s, kc, :], scond[:, kc, :],
                         kc == 0, kc == KC - 1, R=[wm.T(s), scond.T()], W=[ps.T(b)])
                if pc % 8 == 7:
                    p0 = pc - 7
                    K.tt(modv[:, p0:p0 + 8, :], ps[:, b, 0:16].rearrange("p (a c) -> p a c", c=2),
                         vecA[:, p0:p0 + 8].unsqueeze(2).broadcast_to([128, 8, 2]), ALU.add,
                         R=[ps.T(b), vecA.T()], W=[modv.T()])
            for s3 in range(3):
                for kc in range(KC):
                    K.ts(gsv[:, s3, kc, :], modv[:, (3 * s3 + 1) * 8 + kc, :], 1.0, vecA[:, 72 + s3 * 8 + kc:72 + s3 * 8 + kc + 1],
                         ALU.add, ALU.mult, R=[modv.T(), vecA.T()], W=[gsv.T()])
                K.ts(ghv[:, s3], modv[:, (3 * s3 + 2) * 8:(3 * s3 + 3) * 8, :], 0.5 if s3 != 1 else 1.0, None, ALU.mult, None,
                     R=[modv.T()], W=[ghv.T()])

        def norm_to(g, t, scale_fn, bias_fn, out_fn, outW):
            tok0 = g * NT + t * 512
            for kc in range(KC):
                pass
            b = bank()
            for kc in range(KC):
                K.act(sq[:, kc % 2, :], xT[:, kc, tok0:tok0 + 512], AF.Square, R=[xT.T(g, kc, t)], W=[sq.T(kc % 2)],
                      eng="act")
                K.mm(ps[:, b, :], onesb[:], sq[:, kc % 2, :], kc == 0, kc == KC - 1, R=[onesb.T(), sq.T(kc % 2)], W=[ps.T(b)])
            rs = 0
            K.act(tmpf[:, 2, :], ps[:, b, :], AF.Sqrt, R=[ps.T(b), epsb.T()], W=[tmpf.T(2)], bias=epsb[:, 0:1], scale=1.0 / D)
            K.recip(rstd[:, rs, :], tmpf[:, 2, :], R=[tmpf.T(2)], W=[rstd.T(rs)])
            for kc in range(KC):
                tb_ = kc % 2
                K.tt(tmpf[:, tb_, :], xT[:, kc, tok0:tok0 + 512], rstd[:, rs, :], ALU.mult,
                     R=[xT.T(g, kc, t), rstd.T(rs)], W=[tmpf.T(tb_)])
                bi = bias_fn(kc)
                K.act(out_fn(kc), tmpf[:, tb_, :], AF.Identity, R=[tmpf.T(tb_), gsv.T(), modv.T(), vecG.T()], W=outW(kc),
                      scale=scale_fn(kc), bias=bi)

        def mod_norm(g, s3):
            for t in range(2):
                norm_to(g, t,
                        lambda kc: gsv[:, s3, kc, g:g + 1],
                        lambda kc: modv[:, (3 * s3) * 8 + kc, g:g + 1],
                        lambda kc: hT[:, kc, t * 512:(t + 1) * 512],
                        lambda kc: [hT.T(kc, t)])

        def ffn(l, s, g):
            s3 = 0 if s == 0 else 2
            mod_norm(g, s3)
            w1 = ffn_w1[l, s].rearrange("(kc p) n -> p kc n", p=128)
            w2 = ffn_w2[l, s].rearrange("(j p) n -> p j n", p=128)
            for p in range(11):
                def views(sl, p=p):
                    v = wb[:, sl, 0:4096].rearrange("p (kc ab c) -> p kc ab c", kc=KC, ab=2)
                    return [(v[:, :, 0, :], w1[:, :, p * 256:(p + 1) * 256]),
                            (v[:, :, 1, :], w1[:, :, DFF + p * 256:DFF + (p + 1) * 256])]
                sl = load_w(views)
                wv = wb[:, sl, 0:4096].rearrange("p (kc ab c) -> p kc ab c", kc=KC, ab=2)
                for t in range(2):
                    for jj in range(2):
                        j = 2 * p + jj
                        ba = bank()
                        for kc in range(KC):
                            K.mm(ps[:, ba, :], wv[:, kc, 0, jj * 128:(jj + 1) * 128], hT[:, kc, t * 512:(t + 1) * 512],
                                 kc == 0, kc == KC - 1, R=[wb.T(sl), hT.T(kc, t)], W=[ps.T(ba)])
                        bb_ = bank()
                        for kc in range(KC):
                            K.mm(ps[:, bb_, :], wv[:, kc, 1, jj * 128:(jj + 1) * 128], hT[:, kc, t * 512:(t + 1) * 512],
                                 kc == 0, kc == KC - 1, R=[wb.T(sl), hT.T(kc, t)], W=[ps.T(bb_)])
                        tb_ = (K.nbank // 2) % 2
                        K.act(tmpb[:, tb_, :], ps[:, ba, :], AF.Silu, R=[ps.T(ba)], W=[tmpb.T(tb_)])
                        K.tt(hid[:, j, t * 512:(t + 1) * 512], tmpb[:, tb_, :], ps[:, bb_, :], ALU.mult,
                             R=[tmpb.T(tb_), ps.T(bb_)], W=[hid.T(j, t)])
            for p2 in range(4):
                def views2(sl, p2=p2):
                    v = wb[:, sl, 0:JF * 256].rearrange("p (j c) -> p j c", j=JF)
                    return [(v, w2[:, :, p2 * 256:(p2 + 1) * 256])]
                sl = load_w(views2)
                wv = wb[:, sl, 0:JF * 256].rearrange("p (j c) -> p j c", j=JF)
                for t in range(2):
                    for bb in range(2):
                        blk = 2 * p2 + bb
                        b = bank()
                        for j in range(JF):
                            K.mm(ps[:, b, :], wv[:, j, bb * 128:(bb + 1) * 128], hid[:, j, t * 512:(t + 1) * 512],
                                 j == 0, j == JF - 1, R=[wb.T(sl), hid.T(j, t)], W=[ps.T(b)])
                        tok0 = g * NT + t * 512
                        K.stt(xT[:, blk, tok0:tok0 + 512], ps[:, b, :], ghv[:, s3, blk, g:g + 1], xT[:, blk, tok0:tok0 + 512],
                              ALU.mult, ALU.add, R=[ps.T(b), ghv.T(), xT.T(g, blk, t)], W=[xT.T(g, blk, t)])

        w_in = din("w_in", [L, D, 5120])
        w_branch = din("w_branch", [L, 3, 512, D])
        w_merge = din("w_merge", [L, D, 3 * D])
        w_out = din("w_out", [L, D, D])
        vecb_d = din("vecb", [L, 40, 128])
        dlog = din("dlog", [L, 128, 8])
        ctab_d = din("ctab_in", [128, NCT])
        nst = dout("nst", [4 * L * 2 * 4 * 128, 128])
        nk = dout("nk", [4 * L * 8 * 256, 64])
        nv = dout("nv", [4 * L * 8 * 256, 64])

        ctab = K.sbuf("ctab", [128, NCT], F32)
        vecB = K.sbuf("vecB", [128, 40], F32)
        dtb = K.sbuf("dtb", [128, 1760], F32)
        cst = K.sbuf("cst", [128, 4], F32)
        sm = K.sbuf("sm", [128, 64], F32)
        atb = K.sbuf("atb", [128, 4, 128], BF16)
        st32 = K.sbuf("st32", [128, 4, 128], F32)
        K.dma("sp", ctab[:], ctab_d[:, :], R=[], W=[ctab.T()])
        K.memset(cst[:, 0:1], EPS, R=[], W=[cst.T()])
        K.memset(cst[:, 1:2], 1.0, R=[], W=[cst.T()])
        K.memset(cst[:, 2:3], 0.0, R=[], W=[cst.T()])
        K.zero_bias = (cst[:, 2:3], cst.T())
        O_LG, O_DM, O_XF, O_XB, O_ZC, O_G, O_BZ, O_CF, O_CB, O_C0, O_TMP = 0, 8, 520, 1032, 1544, 1552, 1560, 1624, 1640, 1656, 1664
        nbb = [0]

        def bankb():
            nbb[0] += 1
            return nbb[0] % 2

        def layer_tables(l):
            DT = [dtb.T()]
            K.dma("sp", vst[0:40, :], vecb_d[l], R=[], W=[vst.T()])
            b = bank()
            K.tr(ps[:, b, 0:40], vst[0:40, :], identf[0:40, 0:40], R=[vst.T(), identf.T()], W=[ps.T(b)])
            K.cp(vecB[:], ps[:, b, 0:40], R=[ps.T(b)], W=[vecB.T()])
            K.dma("sp", dtb[:, O_TMP:O_TMP + 8], dlog[l], R=[], W=DT)
            K.act(dtb[:, O_TMP + 8:O_TMP + 16], dtb[:, O_TMP:O_TMP + 8], AF.Exp, R=DT, W=DT, scale=-1.0)
            K.act(dtb[:, O_TMP + 16:O_TMP + 24], dtb[:, O_TMP + 8:O_TMP + 16], AF.Ln, R=DT + [cst.T()], W=DT, bias=cst[:, 1:2])
            K.ts(dtb[:, O_LG:O_LG + 8], dtb[:, O_TMP + 16:O_TMP + 24], -1.0, None, ALU.mult, None, R=DT, W=DT)
            CT = [ctab.T()]
            for h in range(4):
                lf = dtb[:, O_LG + h:O_LG + h + 1]
                lb = dtb[:, O_LG + 4 + h:O_LG + 5 + h]
                dm = dtb[:, O_DM + h * 128:O_DM + (h + 1) * 128]
                t0 = dtb[:, O_XF + h * 128:O_XF + (h + 1) * 128]
                t1 = dtb[:, O_XB + h * 128:O_XB + (h + 1) * 128]
                K.act(t0, ctab[:, C_RELF:C_RELF + 128], AF.Exp, R=DT + CT, W=DT, scale=lf)
                K.tt(t0, t0, ctab[:, C_MF:C_MF + 128], ALU.mult, R=DT + CT, W=DT)
                K.act(t1, ctab[:, C_RELB:C_RELB + 128], AF.Exp, R=DT + CT, W=DT, scale=lb)
                K.tt(t1, t1, ctab[:, C_MB:C_MB + 128], ALU.mult, R=DT + CT, W=DT)
                K.tt(dm, t0, t1, ALU.add, R=DT, W=DT)
                K.act(t0, ctab[:, C_IO1:C_IO1 + 128], AF.Exp, R=DT + CT, W=DT, scale=lf)
                K.act(t1, ctab[:, C_IOB:C_IOB + 128], AF.Exp, R=DT + CT, W=DT, scale=lb)
                cc = C_COL
                K.act(dtb[:, O_ZC + h:O_ZC + h + 1], ctab[:, cc + 0:cc + 1], AF.Exp, R=DT + CT, W=DT, scale=lf)
                K.act(dtb[:, O_ZC + 4 + h:O_ZC + 5 + h], ctab[:, cc + 1:cc + 2], AF.Exp, R=DT + CT, W=DT, scale=lb)
                K.act(dtb[:, O_G + h:O_G + h + 1], ctab[:, cc + 19:cc + 20], AF.Exp, R=DT + CT, W=DT, scale=lf)
                K.act(dtb[:, O_G + 4 + h:O_G + 5 + h], ctab[:, cc + 19:cc + 20], AF.Exp, R=DT + CT, W=DT, scale=lb)
                K.act(dtb[:, O_BZ + h * 8:O_BZ + h * 8 + 8], ctab[:, cc + 2:cc + 10], AF.Exp, R=DT + CT, W=DT, scale=lf)
                K.act(dtb[:, O_BZ + 32 + h * 8:O_BZ + 40 + h * 8], ctab[:, cc + 10:cc + 18], AF.Exp, R=DT + CT, W=DT, scale=lb)
                K.act(dtb[:, O_CF + h * 4:O_CF + h * 4 + 4], ctab[:, cc + 20:cc + 24], AF.Exp, R=DT + CT, W=DT, scale=lf)
                K.tt(dtb[:, O_CF + h * 4:O_CF + h * 4 + 4], dtb[:, O_CF + h * 4:O_CF + h * 4 + 4], ctab[:, cc + 24:cc + 28], ALU.mult, R=DT + CT, W=DT)
                K.act(dtb[:, O_CB + h * 4:O_CB + h * 4 + 4], ctab[:, cc + 28:cc + 32], AF.Exp, R=DT + CT, W=DT, scale=lb)
                K.tt(dtb[:, O_CB + h * 4:O_CB + h * 4 + 4], dtb[:, O_CB + h * 4:O_CB + h * 4 + 4], ctab[:, cc + 32:cc + 36], ALU.mult, R=DT + CT, W=DT)
                K.act(dtb[:, O_C0 + h:O_C0 + h + 1], ctab[:, cc + 36:cc + 37], AF.Exp, R=DT + CT, W=DT, scale=lf)
                K.act(dtb[:, O_C0 + 4 + h:O_C0 + 5 + h], ctab[:, cc + 37:cc + 38], AF.Exp, R=DT + CT, W=DT, scale=lb)

        def fm_group(wap_fn, t, b):
            for kc in range(KC):
                K.mm(ps[:, b, :], wap_fn(kc), hT[:, kc, t * 512:(t + 1) * 512], kc == 0, kc == KC - 1,
                     R=[wbT[0], hT.T(kc, t)], W=[ps.T(b)])

        wbT = [None]
        DKS = 128.0 ** -0.5
        SC = 64.0 ** -0.5

        def retention(l, g, win, retT):
            DT = [dtb.T()]
            for h in range(4):
                B = 4096 + (h % 2) * 10240
                qrT, krT, kzf, kzb, vr, qxf, qxb, gsil, Sfa, Sba = [AV(B + i * 1024, 1024) for i in range(10)]

                def views(sl, h=h):
                    v = wb[:, sl, 0:4096].rearrange("p (kc s c) -> p kc s c", kc=KC, s=4)
                    return [(v[:, :, si, :], win[:, :, si * 512 + h * 128:si * 512 + h * 128 + 128]) for si in range(4)]
                sl = load_w(views)
                wbT[0] = wb.T(sl)
                wv = wb[:, sl, 0:4096].rearrange("p (kc s c) -> p kc s c", kc=KC, s=4)
                for t in range(2):
                    for si, dst in ((0, qrT), (1, krT), (3, gsil)):
                        b = bank()
                        fm_group(lambda kc, si=si: wv[:, kc, si, :], t, b)
                        o_ap = dst.ap(t * 512, t * 512 + 512)
                        if si == 3:
                            K.act(o_ap, ps[:, b, :], AF.Silu, R=[ps.T(b)], W=dst.T())
                        elif g == 0:
                            K.act(o_ap, ps[:, b, :], AF.Copy, R=[ps.T(b)], W=dst.T(), scale=DKS if si == 0 else 1.0)
                        else:
                            rope_evac(o_ap, dst, b, t, DKS if si == 0 else 1.0)
                for jb in range(2):
                    b = bank()
                    for jj in range(4):
                        j = jb * 4 + jj
                        for kc in range(KC):
                            K.mm(ps[:, b, jj * 128:(jj + 1) * 128], hT[:, kc, j * 128:(j + 1) * 128], wv[:, kc, 2, :],
                                 kc == 0, kc == KC - 1, R=[wb.T(sl), hT.T(kc, j // 4)], W=[ps.T(b)])
                    K.act(vr.ap(jb * 512, jb * 512 + 512), ps[:, b, :], AF.Copy, R=[ps.T(b)], W=vr.T())
                RL = int(os.environ.get("RDBG", 9))
                if RL < 2:
                    continue
                for jb in range(2):
                    pb = bank()
                    for jj in range(4):
                        j = jb * 4 + jj
                        K.mm(ps[:, pb, jj * 128:(jj + 1) * 128], krT.ap(j * 128, (j + 1) * 128), identb[:], True, True,
                             R=krT.T() + [identb.T()], W=[ps.T(pb)])
                    RS = os.environ.get("RSUB", "da")
                    if "d" in RS:
                        K.ts(kzf.ap(jb * 512, jb * 512 + 512), ps[:, pb, 0:512], dtb[:, O_ZC + h:O_ZC + h + 1], None, ALU.mult, None,
                             R=[ps.T(pb)] + DT, W=kzf.T())
                    if "a" in RS:
                        K.act(kzb.ap(jb * 512, jb * 512 + 512), ps[:, pb, 0:512], AF.Identity, R=[ps.T(pb), cst.T()] + DT, W=kzb.T(),
                              scale=dtb[:, O_ZC + 4 + h:O_ZC + 5 + h], bias=cst[:, 2:3])
                if RL < 3:
                    continue
                for dst, o in ((qxf, O_XF), (qxb, O_XB)):
                    for j in range(8):
                        K.tt(dst.ap(j * 128, j * 128 + 128), qrT.ap(j * 128, j * 128 + 128),
                             dtb[:, o + h * 128:o + (h + 1) * 128], ALU.mult, R=qrT.T() + DT, W=dst.T())
                if RL < 4:
                    continue
                has_f, has_b = set(), set()
                gf = dtb[:, O_G + h:O_G + h + 1]
                gb = dtb[:, O_G + 4 + h:O_G + 5 + h]
                if g == 0:
                    for seq in range(4):
                        j0, j1 = 2 * seq, 2 * seq + 1
                        b = bank()
                        for (ja, jb_, kz, gg, Sall, hs, d) in ((j0, j1, kzf, gf, Sfa, has_f, 0), (j1, j0, kzb, gb, Sba, has_b, 1)):
                            c0 = d * 256
                            K.mm(ps[:, b, c0:c0 + 128], kz.ap(ja * 128, ja * 128 + 128), vr.ap(ja * 128, ja * 128 + 128), True, True,
                                 R=kz.T() + vr.T(), W=[ps.T(b)])
                            K.mm(ps[:, b, c0 + 128:c0 + 256], kz.ap(jb_ * 128, jb_ * 128 + 128), vr.ap(jb_ * 128, jb_ * 128 + 128), True, True,
                                 R=kz.T() + vr.T(), W=[ps.T(b)])
                            K.cp(Sall.ap(jb_ * 128, jb_ * 128 + 128), ps[:, b, c0:c0 + 128], R=[ps.T(b)], W=Sall.T(), eng="act")
                            hs.add(jb_)
                            si_ = (seq % 2) * 2 + d
                            K.cp(st32[:, si_, :], ps[:, b, c0 + 128:c0 + 256], R=[ps.T(b)], W=[st32.T(si_)])
                            K.stt(st32[:, si_, :], ps[:, b, c0:c0 + 128], gg, st32[:, si_, :], ALU.mult, ALU.add,
                                  R=[ps.T(b), st32.T(si_)] + DT, W=[st32.T(si_)])
                            r0_ = (((seq * L + l) * 2 + d) * 4 + h) * 128
                            K.dma("sp", nst[r0_:r0_ + 128, :], st32[:, si_, :], R=[st32.T(si_)], W=[nst.T(seq, l, d, h)])
                else:
                    sample_scan(l, h, kzf, kzb, vr, Sfa, Sba, has_f, has_b, gf, gb)
                if RL < 5:
                    continue
                for j in range(8):
                    r = j % 2
                    b = bank()
                    K.mm(ps[:, b, 0:128], krT.ap(j * 128, j * 128 + 128), qrT.ap(j * 128, j * 128 + 128), True, True,
                         R=krT.T() + qrT.T(), W=[ps.T(b)])
                    K.tt(atb[:, r, :], ps[:, b, 0:128], dtb[:, O_DM + h * 128:O_DM + (h + 1) * 128], ALU.mult,
                         R=[ps.T(b)] + DT, W=[atb.T(r)])
                    nmm = 1 + (j in has_f) + (j in has_b)
                    k_ = 0
                    K.mm(ps[:, b, 128:256], atb[:, r, :], vr.ap(j * 128, j * 128 + 128), True, nmm == 1,
                         R=[atb.T(r)] + vr.T(), W=[ps.T(b)])
                    for (hs, qx, Sall) in ((has_f, qxf, Sfa), (has_b, qxb, Sba)):
                        if j in hs:
                            k_ += 1
                            K.mm(ps[:, b, 128:256], qx.ap(j * 128, j * 128 + 128), Sall.ap(j * 128, j * 128 + 128), False, k_ == nmm - 1,
                                 R=qx.T() + Sall.T(), W=[ps.T(b)])
                    o_ps = ps[:, b, 128:256]
                    s0 = r * 16
                    SM = [sm.T(r)]
                    K.emit("dve", lambda h_, o_ps=o_ps, s0=s0: h_.bn_stats(sm[:, s0:s0 + 6], o_ps), R=[ps.T(b)], W=SM)
                    K.emit("dve", lambda h_, s0=s0: h_.bn_aggr(sm[:, s0 + 6:s0 + 8], sm[:, s0:s0 + 6]), R=SM, W=SM)
                    K.act(sm[:, s0 + 8:s0 + 9], sm[:, s0 + 7:s0 + 8], AF.Sqrt, R=SM + [cst.T()], W=SM, bias=cst[:, 0:1], scale=1.0)
                    K.recip(sm[:, s0 + 9:s0 + 10], sm[:, s0 + 8:s0 + 9], R=SM, W=SM)
                    K.ts(atb[:, 2 + r, :], o_ps, sm[:, s0 + 6:s0 + 7], sm[:, s0 + 9:s0 + 10], ALU.subtract, ALU.mult,
                         R=[ps.T(b)] + SM, W=[atb.T(2 + r)])
                    pb = bank()
                    K.mm(ps[:, pb, 0:128], atb[:, 2 + r, :], identb[:], True, True, R=[atb.T(2 + r), identb.T()], W=[ps.T(pb)])
                    K.tt(retT.ap(h * 1024 + j * 128, h * 1024 + j * 128 + 128), ps[:, pb, 0:128], gsil.ap(j * 128, j * 128 + 128), ALU.mult,
                         R=[ps.T(pb)] + gsil.T(), W=retT.T(h * 1024 + j * 128, h * 1024 + j * 128 + 128))

        def conv(l, g, win, convT):
            zbuf = AV(12288, 1032, fp32=True)
            cbt = AV(12288 + 2064, 1024)
            for c in range(4):
                def views(sl, c=c):
                    v = wb[:, sl, 0:3072].rearrange("p (kc s c) -> p kc s c", kc=KC, s=3)
                    return [(v[:, :, si, :], win[:, :, (4 + si) * 512 + c * 128:(4 + si) * 512 + c * 128 + 128]) for si in range(3)]
                sl = load_w(views)
                wbT[0] = wb.T(sl)
                wv = wb[:, sl, 0:3072].rearrange("p (kc s c) -> p kc s c", kc=KC, s=3)
                if g == 0:
                    zv = zbuf.ap(0, 1032).rearrange("p (s c) -> p s c", s=4)
                    K.memset(zv[:, :, 0:258:257], 0.0, R=[], W=zbuf.T())
                else:
                    sample_conv_pads(zbuf, c)
                for t in range(2):
                    bs = []
                    for si in range(3):
                        b = bank()
                        bs.append(b)
                        fm_group(lambda kc, si=si: wv[:, kc, si, :], t, b)
                    K.act(cbt.ap(t * 512, t * 512 + 512), ps[:, bs[0], :], AF.Copy, R=[ps.T(bs[0])], W=cbt.T())
                    K.act(tmpf[:, 0, :], ps[:, bs[2], :], AF.Copy, R=[ps.T(bs[2])], W=[tmpf.T(0)])
                    if g == 0:
                        K.tt(zv[:, 2 * t:2 * t + 2, 1:257], tmpf[:, 0, :].rearrange("p (s c) -> p s c", s=2),
                             ps[:, bs[1], :].rearrange("p (s c) -> p s c", s=2), ALU.mult,
                             R=[tmpf.T(0), ps.T(bs[1])], W=zbuf.T())
                    else:
                        K.tt(zbuf.ap(1 + t * 512, 1 + t * 512 + 512), tmpf[:, 0, :], ps[:, bs[1], :], ALU.mult,
                             R=[tmpf.T(0), ps.T(bs[1])], W=zbuf.T())
                for t in range(2):
                    if g == 0:
                        zc_ = zv[:, 2 * t:2 * t + 2, 1:257]
                        zl = zv[:, 2 * t:2 * t + 2, 0:256]
                        zr = zv[:, 2 * t:2 * t + 2, 2:258]
                        y = tmpf[:, 1, :].rearrange("p (s c) -> p s c", s=2)
                    else:
                        zc_ = zbuf.ap(1 + t * 512, 513 + t * 512)
                        zl = zbuf.ap(t * 512, 512 + t * 512)
                        zr = zbuf.ap(2 + t * 512, 514 + t * 512)
                        y = tmpf[:, 1, :]
                    VB = [vecB.T()]
                    K.ts(y, zc_, vecB[:, 24 + 4 + c:24 + 5 + c], vecB[:, 36 + c:37 + c], ALU.mult, ALU.add,
                         R=zbuf.T() + VB, W=[tmpf.T(1)])
                    K.stt(y, zl, vecB[:, 24 + c:25 + c], y, ALU.mult, ALU.add, R=zbuf.T() + VB + [tmpf.T(1)], W=[tmpf.T(1)])
                    K.stt(y, zr, vecB[:, 24 + 8 + c:24 + 9 + c], y, ALU.mult, ALU.add, R=zbuf.T() + VB + [tmpf.T(1)], W=[tmpf.T(1)])
                    K.tt(convT.ap(c * 1024 + t * 512, c * 1024 + t * 512 + 512), tmpf[:, 1, :], cbt.ap(t * 512, t * 512 + 512), ALU.mult,
                         R=[tmpf.T(1)] + cbt.T(), W=convT.T(c * 1024 + t * 512, c * 1024 + t * 512 + 512))

        def attention(l, g, win, naT):
            for hh in range(2):
                qnT = AV(12288, 2048)
                knT = AV(14336, 3072)
                vn = AV(17408, 3072)
                Pb = AV(20480, 1024)
                PTb = AV(21504, 1024)
                osb = AV(22528, 256)

                def viewsA(sl, hh=hh):
                    v = wb[:, sl, 0:4096].rearrange("p (kc s c) -> p kc s c", kc=KC, s=2)
                    return [(v[:, :, si, :], win[:, :, (7 + si) * 512 + hh * 256:(7 + si) * 512 + hh * 256 + 256]) for si in range(2)]
                slA = load_w(viewsA)
                wA = wb[:, slA, 0:4096].rearrange("p (kc s c) -> p kc s c", kc=KC, s=2)

                def viewsB(sl, hh=hh):
                    v = wb[:, sl, 0:2048].rearrange("p (kc c) -> p kc c", kc=KC)
                    return [(v, win[:, :, 9 * 512 + hh * 256:9 * 512 + hh * 256 + 256])]
                slB = load_w(viewsB)
                wB = wb[:, slB, 0:2048].rearrange("p (kc c) -> p kc c", kc=KC)
                wbT[0] = wb.T(slA)
                koff = 0 if g == 0 else 256
                kw = 1024 if g == 0 else 1536
                n = 0
                for t in range(2):
                    for cc in range(2):
                        for si, dst, off in ((0, qnT, cc * 1024 + t * 512), (1, knT, cc * kw + koff + t * 512)):
                            b = bank()
                            fm_group(lambda kc, si=si, cc=cc: wA[:, kc, si, cc * 128:(cc + 1) * 128], t, b)
                            K.cp(dst.ap(off, off + 512), ps[:, b, :], R=[ps.T(b)], W=dst.T(off, off + 512), eng="act" if n % 2 else "dve")
                            n += 1
                for j in range(8):
                    b = bank()
                    r = 2
                    if g == 0:
                        for kc in range(KC):
                            K.mm(ps[:, b, 0:256], hT[:, kc, j * 128:(j + 1) * 128], wA[:, kc, 1, :], kc == 0, kc == KC - 1,
                                 R=[wb.T(slA), hT.T(kc, j // 4)], W=[ps.T(b)])
                    for kc in range(KC):
                        K.mm(ps[:, b, 256:512], hT[:, kc, j * 128:(j + 1) * 128], wB[:, kc, :], kc == 0, kc == KC - 1,
                             R=[wb.T(slB), hT.T(kc, j // 4)], W=[ps.T(b)])
                    vo = (j if g == 0 else j + 2) * 256
                    K.cp(vn.ap(vo, vo + 256), ps[:, b, 256:512], R=[ps.T(b)], W=vn.T(vo, vo + 256), eng="act")
                    if g == 0:
                        K.cp(tmpf[:, r, :], ps[:, b, :], R=[ps.T(b)], W=[tmpf.T(r)])
                        seq, t0_ = j // 2, (j % 2) * 128
                        for dd, o in ((nk, 0), (nv, 256)):
                            for h4 in range(4):
                                r0_ = ((seq * L + l) * 8 + 4 * hh + h4) * 256 + t0_
                                K.dma("sp", dd[r0_:r0_ + 128, :], tmpf[:, r, o + h4 * 64:o + h4 * 64 + 64],
                                      R=[tmpf.T(r)], W=[dd.T(seq, l, hh, j)])
                if g == 1:
                    sample_na_prep(l, hh, knT, vn)
                    K.memset(arb[64:128, 22784:22912], 0.0, R=[], W=AV(22784, 128).T())
                    K.memset(arb[0:64, 22912:23040], 0.0, R=[], W=AV(22912, 128).T())
                AL = int(os.environ.get("ADBG", 9))
                if AL < 2:
                    continue
                for j in range(8):
                    reserved.clear()
                    bo = bank()
                    reserved.add(bo)
                    SMT = [sm.T(2)]
                    for hl in range(4):
                        cc, pr = hl // 2, (hl % 2) * 64
                        qap = arb[pr:pr + 64, qnT.off + cc * 1024 + j * 128:qnT.off + cc * 1024 + j * 128 + 128]
                        if g == 0:
                            seq = j // 2
                            b = bank()
                            kap = arb[pr:pr + 64, knT.off + cc * 1024 + seq * 256:knT.off + cc * 1024 + seq * 256 + 256]
                            K.mm(ps[:, b, 0:256], qap, kap, True, True, R=qnT.T() + knT.T(), W=[ps.T(b)])
                            segs = [(b, 0, 256)]
                            vsrc = [(vn, (2 * seq + kb) * 256 + hl * 64) for kb in range(2)]
                        else:
                            segs, vsrc = sample_scores(l, hh, j, hl, cc, pr, qap, knT, vn, qnT)
                        nseg = len(segs)
                        mcol = 32 + hl * 4
                        for i_, (b, c0, c1) in enumerate(segs):
                            K.emit("dve", lambda h_, b=b, c0=c0, c1=c1, mc=mcol + i_: h_.reduce_max(sm[:, mc:mc + 1], ps[:, b, c0:c1], mybir.AxisListType.X),
                                   R=[ps.T(b)], W=SMT)
                        if AL < 3:
                            continue
                        if nseg == 2:
                            K.tt(sm[:, mcol:mcol + 1], sm[:, mcol:mcol + 1], sm[:, mcol + 1:mcol + 2], ALU.max, R=SMT, W=SMT)
                        K.ts(sm[:, mcol + 2:mcol + 3], sm[:, mcol:mcol + 1], -SC, None, ALU.mult, None, R=SMT, W=SMT)
                        po = 0
                        for i_, (b, c0, c1) in enumerate(segs):
                            K.act(Pb.ap(po, po + c1 - c0), ps[:, b, c0:c1], AF.Exp, R=[ps.T(b)] + SMT, W=Pb.T() + SMT,
                                  scale=SC, bias=sm[:, mcol + 2:mcol + 3], accum_out=sm[:, 48 + hl * 2 + i_:48 + hl * 2 + i_ + 1])
                            po += c1 - c0
                        if nseg == 2:
                            K.tt(sm[:, 48 + hl * 2:48 + hl * 2 + 1], sm[:, 48 + hl * 2:48 + hl * 2 + 1], sm[:, 48 + hl * 2 + 1:48 + hl * 2 + 2], ALU.add, R=SMT, W=SMT)
                        if AL < 4:
                            continue
                        nkb = po // 128
                        for k0 in range(0, nkb, 4):
                            pb = bank()
                            k1 = min(nkb, k0 + 4)
                            for kb in range(k0, k1):
                                K.mm(ps[:, pb, (kb - k0) * 128:(kb - k0 + 1) * 128], Pb.ap(kb * 128, (kb + 1) * 128), identb[:], True, True,
                                     R=Pb.T() + [identb.T()], W=[ps.T(pb)])
                            K.cp(PTb.ap(k0 * 128, k1 * 128), ps[:, pb, 0:(k1 - k0) * 128], R=[ps.T(pb)], W=PTb.T(),
                                 eng="dve" if (hl + k0 // 4) % 2 else "act")
                        for kb in range(nkb):
                            vb, vo = vsrc[kb]
                            K.mm(ps[:, bo, hl * 64:(hl + 1) * 64], PTb.ap(kb * 128, (kb + 1) * 128), vb.ap(vo, vo + 64) if vb is not None else vo,
                                 kb == 0, kb == nkb - 1, R=PTb.T() + (vb.T() if vb is not None else [cvb.T()]), W=[ps.T(bo)])
                    reserved.clear()
                    if AL < 5:
                        continue
                    K.recip(sm[:, 56:60], sm[:, 48:56:2], R=SMT, W=SMT)
                    K.tt(osb.ap().rearrange("p (h d) -> p h d", h=4), ps[:, bo, 0:256].rearrange("p (h d) -> p h d", h=4),
                         sm[:, 56:60].unsqueeze(2).broadcast_to([128, 4, 64]), ALU.mult, R=[ps.T(bo)] + SMT, W=osb.T())
                    pb = bank()
                    for cc in range(2):
                        K.mm(ps[:, pb, cc * 128:(cc + 1) * 128], osb.ap(cc * 128, cc * 128 + 128), identb[:], True, True,
                             R=osb.T() + [identb.T()], W=[ps.T(pb)])
                    for cc in range(2):
                        o_ = (2 * hh + cc) * 1024 + j * 128
                        K.cp(naT.ap(o_, o_ + 128), ps[:, pb, cc * 128:(cc + 1) * 128], R=[ps.T(pb)], W=naT.T(o_, o_ + 128),
                             eng="act" if cc else "dve")

        def merge(l, g, retT, convT, naT):
            merged = AV(12288, 8192)
            brs = (retT, convT, naT)
            wmv = w_merge[l].rearrange("(kc p) (b n) -> p kc b n", p=128, b=3)
            for blk in range(8):
                def views(sl, blk=blk):
                    v = wb[:, sl, 0:3072].rearrange("p (kc b c) -> p kc b c", kc=KC, b=3)
                    v2 = wb[:, sl, 3072:4608].rearrange("p (b k c) -> p b k c", b=3, k=4)
                    r_ = [(v[:, :, b3, :], wmv[:, :, b3, blk * 128:(blk + 1) * 128]) for b3 in range(3)]
                    r_ += [(v2[:, b3], w_branch[l, b3].rearrange("(k p) n -> p k n", p=128)[:, :, blk * 128:(blk + 1) * 128]) for b3 in range(3)]
                    return r_
                sl = load_w(views)
                wbT[0] = wb.T(sl)
                wg = wb[:, sl, 0:3072].rearrange("p (kc b c) -> p kc b c", kc=KC, b=3)
                wr = wb[:, sl, 3072:4608].rearrange("p (b k c) -> p b k c", b=3, k=4)
                for t in range(2):
                    for b3 in range(3):
                        bg = bank()
                        fm_group(lambda kc, b3=b3: wg[:, kc, b3, :], t, bg)
                        r = b3 % 2
                        K.act(tmpb[:, r, :], ps[:, bg, :], AF.Sigmoid, R=[ps.T(bg), vecB.T()], W=[tmpb.T(r)],
                              bias=vecB[:, b3 * 8 + blk:b3 * 8 + blk + 1])
                        bw = bank()
                        for k4 in range(4):
                            K.mm(ps[:, bw, :], wr[:, b3, k4, :], brs[b3].ap(k4 * 1024 + t * 512, k4 * 1024 + t * 512 + 512), k4 == 0, k4 == 3,
                                 R=[wb.T(sl)] + brs[b3].T(k4 * 1024 + t * 512, k4 * 1024 + t * 512 + 512), W=[ps.T(bw)])
                        mo = blk * 1024 + t * 512
                        if b3 == 0:
                            K.tt(tmpf[:, 0, :], tmpb[:, r, :], ps[:, bw, :], ALU.mult, R=[tmpb.T(r), ps.T(bw)], W=[tmpf.T(0)])
                        else:
                            K.tt(tmpf[:, 1, :], tmpb[:, r, :], ps[:, bw, :], ALU.mult, R=[tmpb.T(r), ps.T(bw)], W=[tmpf.T(1)])
                            if b3 == 1:
                                K.tt(tmpf[:, 0, :], tmpf[:, 0, :], tmpf[:, 1, :], ALU.add, R=[tmpf.T(0), tmpf.T(1)], W=[tmpf.T(0)])
                            else:
                                K.tt(merged.ap(mo, mo + 512), tmpf[:, 0, :], tmpf[:, 1, :], ALU.add, R=[tmpf.T(0), tmpf.T(1)],
                                     W=merged.T(mo, mo + 512))
            wo = w_out[l].rearrange("(kc p) n -> p kc n", p=128)
            for p4 in range(4):
                def views(sl, p4=p4):
                    v = wb[:, sl, 0:2048].rearrange("p (kc c) -> p kc c", kc=KC)
                    return [(v, wo[:, :, p4 * 256:(p4 + 1) * 256])]
                sl = load_w(views)
                wv = wb[:, sl, 0:2048].rearrange("p (kc c) -> p kc c", kc=KC)
                for t in range(2):
                    for bb in range(2):
                        blk = p4 * 2 + bb
                        b = bank()
                        for kc in range(KC):
                            mo = kc * 1024 + t * 512
                            K.mm(ps[:, b, :], wv[:, kc, bb * 128:(bb + 1) * 128], merged.ap(mo, mo + 512), kc == 0, kc == KC - 1,
                                 R=[wb.T(sl)] + merged.T(mo, mo + 512), W=[ps.T(b)])
                        tok0 = g * NT + t * 512
                        K.stt(xT[:, blk, tok0:tok0 + 512], ps[:, b, :], ghv[:, 1, blk, g:g + 1], xT[:, blk, tok0:tok0 + 512],
                              ALU.mult, ALU.add, R=[ps.T(b), ghv.T(), xT.T(g, blk, t)], W=[xT.T(g, blk, t)])

        if os.environ.get("KDUMP"):
            dbg = Buf(nc.dram_tensor("dbg", [128, 12288], BF16, kind="ExternalOutput").ap())

        def ar_all():
            return [ar.T(k) for k in range(24)]

        def mixer(l, g):
            if g == 1:
                sample_halo_prep(l)
            mod_norm(g, 1)
            win = w_in[l].rearrange("(kc p) n -> p kc n", p=128)
            retT, convT, naT = AV(0, 4096), AV(4096, 4096), AV(8192, 4096)
            DBG_ = DBG if g == 0 else os.environ.get("KDBGS", "rcam")
            if "r" in DBG_:
                retention(l, g, win, retT)
            if "c" in DBG_:
                conv(l, g, win, convT)
            if "a" in DBG_:
                attention(l, g, win, naT)
            if os.environ.get("KDUMP") and l == 0 and g == int(os.environ.get("KDUMP")):
                K.dma("sp", dbg[:, :], arb[:, 0:12288], R=ar_all(), W=[dbg.T()])
            if "m" in DBG_:
                merge(l, g, retT, convT, naT)

        st0 = din("st0", [L, 2, 4, 128, 128])
        ck_d = din("ck", [L, 8, 256, 64])
        cv_d = din("cv", [L, 8, 256, 64])
        ropet_d = din("ropet_in", [128, 2048])
        natab_d = din("natab_in", [128, 1792])
        rpb_d = din("rpb", [L, 8, 15, 31])
        jrev_d = din("jrev_in", [128, 128])
        padinit = din("padinit", [L, 8, 2176])
        padd_h = nc.dram_tensor("padd", [L, 8, 2176], F32, kind="Internal")
        padd = Buf(padd_h.ap())
        CCW = (2048, 2048, CW - 4096)
        ccsB = [Buf(nc.dram_tensor("cc_src%d" % i, [128, CCW[i]], F32, kind="Internal").ap()) for i in range(3)]
        ccdB = [Buf(nc.dram_tensor("cc_dst%d" % i, [512, CCW[i]], F32, kind="Internal").ap()) for i in range(3)]
        ccsT = [b_.T() for b_ in ccsB]
        ccdT = [b_.T() for b_ in ccdB]

        def ccs_ap(c0, c1):
            i = min(c0 // 2048, 2)
            return ccsB[i][:, c0 - 2048 * i:c1 - 2048 * i]

        def ccv_ap(c0, c1):
            i = min(c0 // 2048, 2)
            return ccdB[i][:, :].rearrange("(r p) c -> p r c", p=128)[:, :, c0 - 2048 * i:c1 - 2048 * i]
        ropet = K.sbuf("ropet", [128, 2048], BF16)
        natab = K.sbuf("natab", [128, 1792], BF16)
        jrev = K.sbuf("jrev", [128, 128], BF16)
        Ub = K.sbuf("Ub", [128, 4096], BF16)
        ckT = K.sbuf("ckT", [128, 4, 256], BF16)
        cvb = K.sbuf("cvb", [128, 2, 512], BF16)
        zh = K.sbuf("zh", [128, 48], F32)
        K.dma("pool", ropet[:], ropet_d[:, :], R=[], W=[ropet.T()])
        K.dma("pool", natab[:], natab_d[:, :], R=[], W=[natab.T()])
        K.dma("pool", jrev[:], jrev_d[:, :], R=[], W=[jrev.T()])
        K.dma("sp", padd[:, :, :], padinit[:, :, :], R=[], W=[padd.T()])
        for l_ in range(L):
            K.dma("sp", padd[l_].rearrange("h (r x) -> h r x", r=17)[:, 1:16, 48:79], rpb_d[l_], R=[], W=[padd.T()])
        GROUPS = [[0, 1, 2, 3], [4, 5, 6, 7]]

        def rope_evac(o_ap, dst, b, t, scale):
            K.act(tmpf[:, 0, :], ps[:, b, :], AF.Copy, R=[ps.T(b)], W=[tmpf.T(0)], scale=scale)
            b2 = bank()
            K.mm(ps[:, b2, :], ctab[:, C_SW:C_SW + 128], tmpf[:, 0, :], True, True, R=[ctab.T(), tmpf.T(0)], W=[ps.T(b2)])
            K.tt(tmpf[:, 1, :], tmpf[:, 0, :], ropet[:, t * 512:(t + 1) * 512], ALU.mult, R=[tmpf.T(0), ropet.T()], W=[tmpf.T(1)])
            K.tt(tmpf[:, 2, :], ps[:, b2, :], ropet[:, 1024 + t * 512:1024 + (t + 1) * 512], ALU.mult,
                 R=[ps.T(b2), ropet.T()], W=[tmpf.T(2)])
            K.tt(o_ap, tmpf[:, 1, :], tmpf[:, 2, :], ALU.add, R=[tmpf.T(1), tmpf.T(2)], W=dst.T())

        def v_tokmajor(vr, wv_fn, sl):
            for jb in range(2):
                b = bank()
                for jj in range(4):
                    j = jb * 4 + jj
                    for kc in range(KC):
                        K.mm(ps[:, b, jj * 128:(jj + 1) * 128], hT[:, kc, j * 128:(j + 1) * 128], wv_fn(kc),
                             kc == 0, kc == KC - 1, R=[wb.T(sl), hT.T(kc, j // 4)], W=[ps.T(b)])
                K.act(vr.ap(jb * 512, jb * 512 + 512), ps[:, b, :], AF.Copy, R=[ps.T(b)], W=vr.T())

        def sample_pre(l):
            DT = [dtb.T()]
            win = w_in[l].rearrange("(kc p) n -> p kc n", p=128)
            for h in range(4):
                B = 4096 + (h % 2) * 10240
                krT, kZf, kZb, vr = AV(B + 1024, 1024), AV(B + 2048, 1024), AV(B + 3072, 1024), AV(B + 4096, 1024)

                def views(sl, h=h):
                    v = wb[:, sl, 0:2048].rearrange("p (kc s c) -> p kc s c", kc=KC, s=2)
                    return [(v[:, :, si, :], win[:, :, (1 + si) * 512 + h * 128:(1 + si) * 512 + h * 128 + 128]) for si in range(2)]
                sl = load_w(views)
                wbT[0] = wb.T(sl)
                wv = wb[:, sl, 0:2048].rearrange("p (kc s c) -> p kc s c", kc=KC, s=2)
                for t in range(2):
                    b = bank()
                    fm_group(lambda kc: wv[:, kc, 0, :], t, b)
                    rope_evac(krT.ap(t * 512, t * 512 + 512), krT, b, t, 1.0)
                v_tokmajor(vr, lambda kc: wv[:, kc, 1, :], sl)
                for jb in range(2):
                    pb = bank()
                    for jj in range(4):
                        j = jb * 4 + jj
                        K.mm(ps[:, pb, jj * 128:(jj + 1) * 128], krT.ap(j * 128, (j + 1) * 128), identb[:], True, True,
                             R=krT.T() + [identb.T()], W=[ps.T(pb)])
                    for kz, o in ((kZf, O_BZ + h * 8 + jb * 4), (kZb, O_BZ + 32 + h * 8 + jb * 4)):
                        K.tt(kz.ap(jb * 512, jb * 512 + 512).rearrange("p (j c) -> p j c", j=4),
                             ps[:, pb, :].rearrange("p (j c) -> p j c", j=4),
                             dtb[:, o:o + 4].unsqueeze(2).broadcast_to([128, 4, 128]), ALU.mult,
                             R=[ps.T(pb)] + DT, W=kz.T())
                b = bank()
                for d, kz in ((0, kZf), (1, kZb)):
                    for j in range(8):
                        K.mm(ps[:, b, d * 128:(d + 1) * 128], kz.ap(j * 128, j * 128 + 128), vr.ap(j * 128, j * 128 + 128), j == 0, j == 7,
                             R=kz.T() + vr.T(), W=[ps.T(b)])
                K.cp(tmpf[:, 2, 0:256], ps[:, b, 0:256], R=[ps.T(b)], W=[tmpf.T(2)])
                for d in range(2):
                    K.dma("sp", ccs_ap((d * 4 + h) * 128, (d * 4 + h) * 128 + 128), tmpf[:, 2, d * 128:(d + 1) * 128],
                          R=[tmpf.T(2)], W=ccsT)
            for si_, base in ((8, 1024), (9, 3072)):
                def views(sl, si_=si_):
                    v = wb[:, sl, 0:4096].rearrange("p (kc c) -> p kc c", kc=KC)
                    return [(v, win[:, :, si_ * 512:(si_ + 1) * 512])]
                sl = load_w(views)
                wv = wb[:, sl, 0:4096].rearrange("p (kc c) -> p kc c", kc=KC)
                for c in range(4):
                    b = bank()
                    if si_ == 8:
                        for ri, tok0 in enumerate((0, 768)):
                            for kc in range(KC):
                                K.mm(ps[:, b, ri * 256:(ri + 1) * 256], wv[:, kc, c * 128:(c + 1) * 128], hT[:, kc, tok0:tok0 + 256],
                                     kc == 0, kc == KC - 1, R=[wb.T(sl), hT.T(kc, ri)], W=[ps.T(b)])
                    else:
                        j = (0, 1, 6, 7)[c]
                        for kc in range(KC):
                            K.mm(ps[:, b, :], hT[:, kc, j * 128:(j + 1) * 128], wv[:, kc, :], kc == 0, kc == KC - 1,
                                 R=[wb.T(sl), hT.T(kc, j // 4)], W=[ps.T(b)])
                    K.cp(tmpf[:, c % 2, :], ps[:, b, :], R=[ps.T(b)], W=[tmpf.T(c % 2)], eng="act" if c % 2 else "dve")
                    K.dma("sp", ccs_ap(base + c * 512, base + (c + 1) * 512), tmpf[:, c % 2, :], R=[tmpf.T(c % 2)], W=ccsT)
            sls = []
            for si_ in (5, 6):
                def views(sl, si_=si_):
                    v = wb[:, sl, 0:4096].rearrange("p (kc c) -> p kc c", kc=KC)
                    return [(v, win[:, :, si_ * 512:(si_ + 1) * 512])]
                sls.append(load_w(views))
            b = bank()
            for c in range(4):
                for q_, sl in enumerate(sls):
                    wv = wb[:, sl, 0:4096].rearrange("p (kc c) -> p kc c", kc=KC)
                    for kc in range(KC):
                        K.mm(ps[:, b, c * 4 + q_ * 2:c * 4 + q_ * 2 + 2], wv[:, kc, c * 128:(c + 1) * 128], hT[:, kc, 0:1024:1023],
                             kc == 0, kc == KC - 1, R=[wb.T(sl), hT.T(kc, 0), hT.T(kc, 1)], W=[ps.T(b)])
            K.cp(sm[:, 0:16], ps[:, b, 0:16], R=[ps.T(b)], W=[sm.T(0)])
            smv = sm[:, 0:16].rearrange("p (c s) -> p c s", c=4)
            K.tt(tmpf[:, 2, 256:264].rearrange("p (c s) -> p c s", c=4), smv[:, :, 0:2], smv[:, :, 2:4], ALU.mult,
                 R=[sm.T(0)], W=[tmpf.T(2)])
            K.dma("sp", ccs_ap(5120, 5128), tmpf[:, 2, 256:264], R=[tmpf.T(2)], W=ccsT)
            for i_ in range(3):
                K.collective(ccsB[i_][:, :], ccdB[i_][:, :], GROUPS, R=[ccsT[i_]], W=[ccdT[i_]],
                             flag_ap=zh[:, 40 + i_:41 + i_], flag_tile=zh.T("flag"))

        def sample_scan(l, h, kzf, kzb, vr, Sfa, Sba, has_f, has_b, gf, gb):
            DT = [dtb.T()]
            for d, (kz, gg, Sall, hs, order) in enumerate(((kzf, gf, Sfa, has_f, list(range(8))),
                                                           (kzb, gb, Sba, has_b, list(range(7, -1, -1))))):
                sl4 = tmpf[:, d, :].rearrange("p (r c) -> p r c", r=4)
                K.dma("sp", sl4, ccv_ap((d * 4 + h) * 128, (d * 4 + h) * 128 + 128), R=ccdT, W=[tmpf.T(d)])
                K.dma("sp", st32[:, d, :], st0[l, d, h], R=[], W=[st32.T(d)])
                S = st32[:, 2 + d, :]
                ST = [st32.T(2 + d)]
                K.ts(S, st32[:, d, :], dtb[:, O_C0 + d * 4 + h:O_C0 + d * 4 + h + 1], None, ALU.mult, None, R=[st32.T(d)] + DT, W=ST)
                cofs = O_CF if d == 0 else O_CB
                for i in range(4):
                    K.stt(S, sl4[:, i, :], dtb[:, cofs + h * 4 + i:cofs + h * 4 + i + 1], S, ALU.mult, ALU.add,
                          R=[tmpf.T(d)] + ST + DT, W=ST)
                bks = [bank(), bank()]
                for j in range(8):
                    K.mm(ps[:, bks[j // 4], (j % 4) * 128:(j % 4 + 1) * 128], kz.ap(j * 128, j * 128 + 128), vr.ap(j * 128, j * 128 + 128),
                         True, True, R=kz.T() + vr.T(), W=[ps.T(bks[j // 4])])
                for j in order:
                    K.cp(Sall.ap(j * 128, j * 128 + 128), S, R=ST, W=Sall.T(), eng="act")
                    hs.add(j)
                    K.ts(S, S, gg, None, ALU.mult, None, R=ST + DT, W=ST)
                    K.tt(S, S, ps[:, bks[j // 4], (j % 4) * 128:(j % 4 + 1) * 128], ALU.add,
                         R=ST + [ps.T(bks[j // 4])], W=ST)

        def sample_halo_prep(l):
            K.dma("sp", zh[:, 0:32].rearrange("p (r c) -> p r c", r=4), ccv_ap(5120, 5128), R=ccdT, W=[zh.T()])
            zgv = zh[:, 0:32].rearrange("p (r c s) -> p r c s", r=4, c=4)
            for selbase, s_idx, ocol in ((38, 1, 0), (42, 0, 1)):
                out = zh[:, 32:40].rearrange("p (c s) -> p c s", s=2)[:, :, ocol]
                K.ts(out, zgv[:, 0, :, s_idx], ctab[:, C_COL + selbase:C_COL + selbase + 1], None, ALU.mult, None,
                     R=[zh.T(), ctab.T()], W=[zh.T()])
                for i in range(1, 4):
                    K.stt(out, zgv[:, i, :, s_idx], ctab[:, C_COL + selbase + i:C_COL + selbase + i + 1], out, ALU.mult, ALU.add,
                          R=[zh.T(), ctab.T()], W=[zh.T()])
            for tile in range(2):
                K.dma("pool", cvb[:, tile, :].rearrange("p (h d) -> p h d", h=8),
                      cv_d[l][:, tile * 128:(tile + 1) * 128, :].rearrange("h t d -> t h d"), R=[], W=[cvb.T()])
                K.dma("sp", stg[:, tile, 0:512].rearrange("p (h d) -> p h d", h=8),
                      ck_d[l][:, tile * 128:(tile + 1) * 128, :].rearrange("h t d -> t h d"), R=[], W=[stg.T(tile)])
                b = bank()
                for c in range(4):
                    K.tr(ps[:, b, c * 128:(c + 1) * 128], stg[:, tile, c * 128:(c + 1) * 128], identf[:],
                         R=[stg.T(tile), identf.T()], W=[ps.T(b)])
                K.cp(ckT[:, :, tile * 128:(tile + 1) * 128], ps[:, b, :].rearrange("p (c t) -> p c t", c=4), R=[ps.T(b)], W=[ckT.T()])

        def sample_conv_pads(zbuf, c):
            K.cp(zbuf.ap(0, 1), zh[:, 32 + 2 * c:33 + 2 * c], R=[zh.T()], W=zbuf.T())
            K.cp(zbuf.ap(1025, 1026), zh[:, 33 + 2 * c:34 + 2 * c], R=[zh.T()], W=zbuf.T())

        def sample_na_prep(l, hh, knT, vn):
            for a in range(2):
                for h4 in range(4):
                    src = bass.AP(padd_h, l * 8 * 2176 + (4 * hh + h4) * 2176 + (1 - a) * 128, [[1, 64], [128, 16], [1, 64]])
                    K.dma("pool", Ub[a * 64:(a + 1) * 64, h4 * 1024:(h4 + 1) * 1024].rearrange("p (j c) -> p j c", j=16), src,
                          R=[padd.T()], W=[Ub.T()])
            K.stt(Ub[:].rearrange("p (x c) -> p x c", c=64), Ub[:].rearrange("p (x c) -> p x c", c=64), 1.0 / SC,
                  ctab[:, C_COL + 46:C_COL + 110].unsqueeze(1).broadcast_to([128, 64, 64]), ALU.mult, ALU.add,
                  R=[Ub.T(), ctab.T()], W=[Ub.T()])
            stgv = tmpf[:, 0:2, :].rearrange("p a (b c) -> p (a b) c", b=2)
            acc = tmpf[:, 2, 0:256]
            TT = [tmpf.T(0), tmpf.T(1)]

            def combine(col, selbase, out_ap, outW):
                K.dma("sp", stgv, ccv_ap(col, col + 256), R=ccdT, W=TT)
                K.ts(acc, stgv[:, 0, :], ctab[:, C_COL + selbase:C_COL + selbase + 1], None, ALU.mult, None, R=TT + [ctab.T()], W=[tmpf.T(2)])
                for i in (1, 2):
                    K.stt(acc, stgv[:, i, :], ctab[:, C_COL + selbase + i:C_COL + selbase + i + 1], acc, ALU.mult, ALU.add,
                          R=TT + [ctab.T(), tmpf.T(2)], W=[tmpf.T(2)])
                K.stt(out_ap, stgv[:, 3, :], ctab[:, C_COL + selbase + 3:C_COL + selbase + 4], acc, ALU.mult, ALU.add,
                      R=TT + [ctab.T(), tmpf.T(2)], W=outW)
            for cc in range(2):
                c = 2 * hh + cc
                combine(1024 + c * 512 + 256, 38, knT.ap(cc * 1536, cc * 1536 + 256), knT.T(cc * 1536, cc * 1536 + 256))
                combine(1024 + c * 512, 42, knT.ap(cc * 1536 + 1280, cc * 1536 + 1536), knT.T(cc * 1536 + 1280, cc * 1536 + 1536))
            for idx, selbase, tile in ((2, 38, 0), (3, 38, 1), (0, 42, 10), (1, 42, 11)):
                combine(3072 + idx * 512 + hh * 256, selbase, vn.ap(tile * 256, tile * 256 + 256), vn.T(tile * 256, tile * 256 + 256))

        def sample_scores(l, hh, m, hl, cc, pr, qap, knT, vn, qnT):
            pstart = m if m <= 6 else 6
            np_ = 6 if m in (0, 7) else 5
            nk_ = np_ * 128
            j0 = 3 if m <= 6 else 1
            kbase = knT.off + cc * 1536 + pstart * 128
            b0, b1 = bank(), bank()
            w1 = nk_ - 512
            qz = AV(22784 + (pr // 64) * 128, 128)
            K.cp(arb[pr:pr + 64, qz.off:qz.off + 128], qap, R=qnT.T(), W=qz.T())
            qfull = arb[:, qz.off:qz.off + 128]
            for (b, lo, hi) in ((b0, 0, 512), (b1, 512, nk_)):
                wd = hi - lo
                K.mm(ps[:, b, 0:wd], qfull, arb[:, kbase + lo:kbase + hi], True, False, R=qz.T() + knT.T(), W=[ps.T(b)])
                K.mm(ps[:, b, 0:wd], jrev[:], Ub[:, hl * 1024 + j0 * 64 + lo:hl * 1024 + j0 * 64 + hi], False, False,
                     R=[jrev.T(), Ub.T()], W=[ps.T(b)])
                K.mm(ps[:, b, 0:wd], natab[:, m * 128:(m + 1) * 128], natab[:, 1024 + lo:1024 + hi], False, True,
                     R=[natab.T()], W=[ps.T(b)])
            cg = 2 * hh + cc
            K.mm(ps[:, b1, w1:w1 + 256], qfull, ckT[:, cg, :], True, True, R=qz.T() + [ckT.T()], W=[ps.T(b1)])
            segs = [(b0, 0, 512), (b1, 0, w1 + 256)]
            vsrc = [(vn, (pstart + kb) * 256 + hl * 64) for kb in range(np_)]
            vsrc += [(None, cvb[:, tile, (4 * hh + hl) * 64:(4 * hh + hl) * 64 + 64]) for tile in range(2)]
            return segs, vsrc

        DBG = os.environ.get("KDBG", "rcam")
        for l in range(int(os.environ.get("KL", L))):
            layer_vectors(l)
            layer_tables(l)
            ffn(l, 0, 0)
            ffn(l, 0, 1)
            if STAGE == "full":
                mod_norm(1, 1)
                sample_pre(l)
            mixer(l, 0)
            if STAGE == "full" and os.environ.get("KDBGS", "rcam") != "none":
                mixer(l, 1)
            ffn(l, 1, 0)
            ffn(l, 1, 1)

        yTf_ap = ar.t[:, 0:KC * 512].rearrange("p (k t) -> p k t", k=KC)

        class _Y:
            def __getitem__(self, idx):
                return yTf_ap[idx]

            def T(self, kc):
                return ar.T(kc)
        yTf = _Y()
        for g, dst in ((0, yp), (1, ys)):
            for t in range(2):
                norm_to(g, t,
                        lambda kc: vecG[:, 16 + kc:17 + kc],
                        lambda kc: 0.0,
                        lambda kc: yTf[:, kc, :],
                        lambda kc: [yTf.T(kc)])
                for tb in range(4):
                    sl = ld % 2
                    ld += 1
                    for hf in range(2):
                        b = bank()
                        for q in range(4):
                            kc = hf * 4 + q
                            K.tr(ps[:, b, q * 128:(q + 1) * 128], yTf[:, kc, tb * 128:(tb + 1) * 128], identf[:],
                                 R=[yTf.T(kc), identf.T()], W=[ps.T(b)])
                        K.cp(stg[:, sl, hf * 512:(hf + 1) * 512], ps[:, b, :], R=[ps.T(b)], W=[stg.T(sl)],
                             eng="act" if hf else "dve")
                    r0 = t * 512 + tb * 128
                    K.dma("sp", dst[r0:r0 + 128, :], stg[:, sl, :], R=[stg.T(sl)], W=[dst.T(r0)])

        K.finish()
        with nc.Block() as block:
            K.replay(block)
    return nc


_NC_CACHE = {}


def make_ctab(qd):
    f = np.float32
    t = np.zeros((128, NCT), f)
    p = np.arange(128)
    t[:, 0:128] = np.eye(128, dtype=f)
    perm = np.where((p % 64) < 32, p + 32, p - 32)
    sw = np.zeros((128, 128), f)
    sw[perm, p] = 1.0
    t[:, C_SW:C_SW + 128] = sw
    k = p[:, None]
    q = p[None, :]
    t[:, C_RELF:C_RELF + 128] = np.maximum(q - k, 0)
    t[:, C_MF:C_MF + 128] = (q >= k)
    t[:, C_RELB:C_RELB + 128] = np.maximum(k - q, 0)
    t[:, C_MB:C_MB + 128] = (k >= q)
    t[:, C_IO1:C_IO1 + 128] = (p + 1)[None, :]
    t[:, C_IOB:C_IOB + 128] = (128 - p)[None, :]
    c = C_COL
    t[:, c + 0] = 127 - p
    t[:, c + 1] = p
    for j in range(8):
        t[:, c + 2 + j] = 1023 - 128 * j - p
        t[:, c + 10 + j] = 128 * j + p
    t[:, c + 18] = 255 - p
    t[:, c + 19] = 128
    for i in range(4):
        t[:, c + 20 + i] = 1024 * max(qd - 1 - i, 0)
        t[:, c + 24 + i] = float(i < qd)
        t[:, c + 28 + i] = 1024 * max(i - qd - 1, 0)
        t[:, c + 32 + i] = float(i > qd)
        t[:, c + 38 + i] = float(i == qd - 1)
        t[:, c + 42 + i] = float(i == qd + 1)
    t[:, c + 36] = 1024 * qd
    t[:, c + 37] = 1024 * (3 - qd)
    cq = np.arange(64)
    c0 = np.clip(cq - 8, 0, 48)
    ck = np.arange(64)
    ok = (ck[None, :] >= c0[:, None]) & (ck[None, :] < c0[:, None] + 16)
    cm = np.where(ok, 0.0, NEG).astype(f)
    t[0:64, c + 46:c + 110] = cm[::-1]
    t[64:128, c + 46:c + 110] = cm[::-1]
    return t


def kernel(**inp):
    f = np.float32
    nc = _NC_CACHE.get("nc")
    if nc is None:
        nc = _NC_CACHE["nc"] = build_program()
    ident = np.eye(128, dtype=f)
    vecb = np.ascontiguousarray(np.concatenate([inp["b_merge"].reshape(L, 24, 128), inp["conv_w"].reshape(L, 12, 128),
                                                inp["conv_b"].reshape(L, 4, 128)], 1).astype(f))
    dlog = np.ascontiguousarray(np.broadcast_to(inp["ret_decay_logit"].reshape(L, 1, 8), (L, 128, 8)).astype(f))
    in_maps = []
    for c in range(8):
        b, qd = c // 4, c % 4
        m = {
            "xp": np.ascontiguousarray(inp["x_prompt"][4 * c:4 * c + 4].reshape(NT, D)),
            "xs": np.ascontiguousarray(inp["x_sample"][b, qd * NT:(qd + 1) * NT]),
            "vecg": np.ascontiguousarray(np.concatenate([inp["c_ctx"].reshape(8, 128), inp["c"][b].reshape(8, 128),
                                                         inp["final_g"].reshape(8, 128)], 0).astype(f)),
            "ident": ident,
            "norm_g": np.ascontiguousarray(inp["norm_g"].reshape(L, 24, 128)),
            "w_mod": inp["w_mod"],
            "b_mod": np.ascontiguousarray(inp["b_mod"].reshape(L, 72, 128)),
            "ffn_w1": inp["ffn_w1"],
            "ffn_w2": inp["ffn_w2"],
            "w_in": inp["w_in"],
            "w_branch": inp["w_branch"],
            "w_merge": inp["w_merge"],
            "w_out": inp["w_out"],
            "vecb": vecb,
            "dlog": dlog,
            "ctab_in": make_ctab(qd),
        }
        m.update(sample_inputs(inp, b, qd))
        in_maps.append(m)
    res = run_bass_kernel_spmd(nc, in_maps, core_ids=list(range(8)))
    r = res.results
    y_prompt = np.concatenate([r[c]["yp"].reshape(4, 256, D) for c in range(8)], 0)
    y_sample = np.stack([np.concatenate([r[4 * b + q]["ys"] for q in range(4)], 0) for b in range(2)], 0)
    nst = np.concatenate([r[c]["nst"].reshape(4, L, 2, 4, 128, 128) for c in range(8)], 0)
    nk = np.concatenate([r[c]["nk"].reshape(4, L, 8, 256, 64) for c in range(8)], 0)
    nv = np.concatenate([r[c]["nv"].reshape(4, L, 8, 256, 64) for c in range(8)], 0)
    return y_prompt, y_sample, nst, nk, nv


def sample_inputs(inp, b, qd):
    f = np.float32
    p = np.arange(128)
    t = np.arange(1024) + qd * 1024
    i = (p % 64) % 32
    inv = (np.float32(10000.0) ** (-(i.astype(f)) * f(2.0) / f(64.0))).astype(f)
    pos = np.where(p[:, None] < 64, (t // 64)[None, :], (t % 64)[None, :]).astype(f)
    ang = (pos * inv[:, None]).astype(f)
    sign = np.where((p % 64) < 32, -1.0, 1.0).astype(f)[:, None]
    ropet = np.concatenate([np.cos(ang), np.sin(ang) * sign], 1).astype(f)
    natab = np.zeros((128, 1792), f)
    for m in range(8):
        pstart = m if m <= 6 else 6
        for a in range(2):
            i_ = 2 * m + a
            r = 16 * qd + i_
            r0 = min(max(r - 4, 0), 56)
            for s_ in range(12):
                grow = 16 * qd - 4 + 2 * pstart + s_
                ok = (r0 <= grow < r0 + 8)
                natab[s_, m * 128 + a * 64:m * 128 + a * 64 + 64] = 0.0 if ok else NEG
    for s_ in range(12):
        natab[s_, 1024 + s_ * 64:1024 + (s_ + 1) * 64] = 1.0
    jrev = np.zeros((128, 128), f)
    for a in range(2):
        for cq in range(64):
            jrev[a * 64 + 63 - cq, a * 64 + cq] = 1.0
    padinit = np.zeros((L, 8, 17, 128), f)
    padinit[:, :, 0, :] = NEG
    padinit[:, :, 16, :] = NEG
    return {
        "st0": np.ascontiguousarray(inp["state_ret"][b]),
        "ck": np.ascontiguousarray(inp["cache_na_k"][b]),
        "cv": np.ascontiguousarray(inp["cache_na_v"][b]),
        "ropet_in": ropet,
        "natab_in": natab,
        "rpb": inp["na_rpb"],
        "jrev_in": jrev,
        "padinit": padinit.reshape(L, 8, 2176),
    }
```

```python
import contextlib
import os
import numpy as np
import concourse.bass as bass
import concourse.mybir as mybir
from concourse.bass_utils import run_bass_kernel_spmd

F32 = mybir.dt.float32
BF16 = mybir.dt.bfloat16
AF = mybir.ActivationFunctionType
ALU = mybir.AluOpType

D = 1024
KC = 8
NT = 1024
L = 4
DFF = 2816
JF = 22
EPS = 1e-6
NEG = -1e30

STAGE = os.environ.get("KSTAGE", "full")
C_ID, C_SW, C_RELF, C_MF, C_RELB, C_MB, C_IO1, C_IOB, C_COL = 0, 128, 256, 384, 512, 640, 768, 896, 1024
NCT = 1024 + 110
CW = 5128


class Tile:
    __slots__ = ("w", "r", "ex")

    def __init__(self):
        self.w = None
        self.r = {}
        self.ex = False


class Eng:
    def __init__(self, name, sem):
        self.name = name
        self.sem = sem
        self.cnt = 0
        self.waited = {}
        self.ops = []


class Buf:
    def __init__(self, t, ex=False):
        self.t = t
        self.tiles = {}
        self.ex = ex

    def T(self, *key):
        tl = self.tiles.get(key)
        if tl is None:
            tl = self.tiles[key] = Tile()
            tl.ex = self.ex
        return tl

    def __getitem__(self, idx):
        return self.t[idx]


class KB:
    def __init__(self, nc, stack):
        self.nc = nc
        self.stack = stack
        self.semc = 0
        self.E = {}
        for n in ("pe", "act", "dve", "pool", "sp"):
            self.E[n] = Eng(n, self.newsem("e_" + n))
        self.dsem = {}
        for q in ("sp", "act", "pool"):
            self.dsem[q] = [[self.newsem("d_%s%d" % (q, i)), 0] for i in range(6)]
        self.drr = {"sp": 0, "act": 0, "pool": 0}
        self.allsigs = {}
        self.nbank = 0
        self.zero_bias = None
        self.ccsem = None
        self.ccn = 0

    def newsem(self, name):
        self.semc += 1
        return self.stack.enter_context(self.nc.semaphore(name))

    def sbuf(self, name, shape, dt):
        return Buf(self.stack.enter_context(self.nc.sbuf_tensor(name, shape, dt)))

    def psum(self, name, shape, dt):
        return Buf(self.stack.enter_context(self.nc.psum_tensor(name, shape, dt)), ex=True)

    def _deps(self, R, W):
        deps = {}
        exr = [t for t in R if t.ex]
        if exr:
            W = list(W) + exr

        def add(sig):
            if sig is None:
                return
            s, v = sig
            k = id(s)
            if k not in deps or deps[k][1] < v:
                deps[k] = (s, v)

        for t in R:
            add(t.w)
        for t in W:
            add(t.w)
            for sig in t.r.values():
                add(sig)
        return deps

    def _mark(self, sig, R, W):
        k = id(sig[0])
        exr = [t for t in R if t.ex]
        if exr:
            W = list(W) + exr
            R = [t for t in R if not t.ex]
        for t in R:
            t.r[k] = sig
        for t in W:
            t.w = sig
            t.r = {}
        self.allsigs[k] = sig

    def emit(self, en, fn, R=(), W=()):
        e = self.E[en]
        deps = self._deps(R, W)
        waits = []
        for k, (s, v) in deps.items():
            if e.waited.get(k, 0) >= v:
                continue
            if en == "pe" and s is e.sem:
                continue
            e.waited[k] = v
            waits.append((s, v))
        e.cnt += 1
        sig = (e.sem, e.cnt)
        e.waited[id(e.sem)] = max(e.waited.get(id(e.sem), 0), 0)
        e.ops.append((waits, fn, e.sem, 1))
        self._mark(sig, R, W)

    def dma(self, q, out, in_, R=(), W=(), **kw):
        e = self.E[q]
        slot = self.dsem[q][self.drr[q] % len(self.dsem[q])]
        self.drr[q] += 1
        deps = self._deps(R, W)
        waits = []
        if slot[1] > 0:
            deps[id(slot[0])] = (slot[0], slot[1])
        for k, (s, v) in deps.items():
            if e.waited.get(k, 0) >= v:
                continue
            e.waited[k] = v
            waits.append((s, v))
        slot[1] += 16
        sig = (slot[0], slot[1])

        def fn(h, out=out, in_=in_, kw=kw):
            return h.dma_start(out=out, in_=in_, **kw)

        e.ops.append((waits, fn, slot[0], 16))
        self._mark(sig, R, W)

    def collective(self, src, dst, groups, R, W, flag_ap, flag_tile):
        e = self.E["pool"]
        csem = self.newsem("ccsem%d" % self.ccn)
        self.ccn += 1
        deps = self._deps(R, W)
        waits = []
        for k, (s_, v) in deps.items():
            if e.waited.get(k, 0) >= v:
                continue
            e.waited[k] = v
            waits.append((s_, v))

        def fn(h):
            return h.collective_compute("AllGather", ALU.bypass, replica_groups=groups, ins=[src], outs=[dst])

        e.ops.append((waits, fn, csem, 1))
        e.cnt += 1
        sig = (e.sem, e.cnt)
        e.ops.append(([(csem, 1)], lambda h: h.memset(flag_ap, 0.0), e.sem, 1))
        self._mark(sig, R, list(W) + [flag_tile])

    def finish(self):
        e = self.E["sp"]
        waits = []
        for k, (s, v) in self.allsigs.items():
            waits.append((s, v))
        e.ops.append((waits, None, None, 0))

    def replay(self, block):
        nc = self.nc
        names = {"pe": "tensor", "act": "scalar", "dve": "vector", "pool": "gpsimd", "sp": "sync"}
        for en, bn in names.items():
            ops = self.E[en].ops

            def body(h, ops=ops):
                for waits, fn, sem, inc in ops:
                    for s, v in waits:
                        h.wait_ge(s, v)
                    if fn is not None:
                        ins = fn(h)
                        ins.then_inc(sem, inc)

            getattr(block, bn)(body)

    def mm(self, out, lhsT, rhs, start, stop, R, W):
        self.emit("pe", lambda h: h.matmul(out, lhsT, rhs, start=start, stop=stop), R, W)

    def tr(self, out, in_, ident, R, W):
        self.emit("pe", lambda h: h.transpose(out, in_, ident), R, W)

    def act(self, out, in_, func, R, W, bias=None, scale=None, accum_out=None, eng="act"):
        kw = {}
        if bias is not None:
            kw["bias"] = bias
        if scale is not None:
            kw["scale"] = scale
        if accum_out is not None:
            kw["accum_out"] = accum_out
        if scale is not None and not isinstance(scale, float) and bias is None and self.zero_bias is not None:
            kw["bias"] = self.zero_bias[0]
            R = list(R) + [self.zero_bias[1]]
        self.emit(eng, lambda h: h.activation(out, in_, func, **kw), R, W)

    def tt(self, out, in0, in1, op, R, W, eng="dve"):
        self.emit(eng, lambda h: h.tensor_tensor(out, in0, in1, op), R, W)

    def ts(self, out, in0, s1, s2, op0, op1, R, W, eng="dve"):
        if op1 is None:
            self.emit(eng, lambda h: h.tensor_scalar(out, in0, s1, None, op0), R, W)
        else:
            self.emit(eng, lambda h: h.tensor_scalar(out, in0, s1, s2, op0, op1), R, W)

    def stt(self, out, in0, scalar, in1, op0, op1, R, W):
        self.emit("dve", lambda h: h.scalar_tensor_tensor(out, in0, scalar, in1, op0, op1), R, W)

    def cp(self, out, in_, R, W, eng="dve"):
        if eng == "act":
            self.emit(eng, lambda h: h.activation(out, in_, AF.Copy), R, W)
        else:
            self.emit(eng, lambda h: h.tensor_copy(out, in_), R, W)

    def recip(self, out, in_, R, W):
        self.emit("dve", lambda h: h.reciprocal(out, in_), R, W)

    def memset(self, out, val, R, W, eng="dve"):
        self.emit(eng, lambda h: h.memset(out, val), R, W)


def build_program():
    nc = bass.Bass("TRN2", target_bir_lowering=False)
    stack = contextlib.ExitStack()
    with stack:
        K = KB(nc, stack)

        def din(name, shape):
            return Buf(nc.dram_tensor(name, list(shape), F32, kind="ExternalInput").ap())

        def dout(name, shape):
            return Buf(nc.dram_tensor(name, list(shape), F32, kind="ExternalOutput").ap())

        xp = din("xp", [NT, D])
        xs = din("xs", [NT, D])
        vecg = din("vecg", [24, 128])
        ident_d = din("ident", [128, 128])
        norm_g = din("norm_g", [L, 24, 128])
        w_mod = din("w_mod", [L, D, 9 * D])
        b_mod = din("b_mod", [L, 72, 128])
        ffn_w1 = din("ffn_w1", [L, 2, D, 2 * DFF])
        ffn_w2 = din("ffn_w2", [L, 2, DFF, D])
        yp = dout("yp", [NT, D])
        ys = dout("ys", [NT, D])

        xT = K.sbuf("xT", [128, KC, 2 * NT], F32)
        hT = K.sbuf("hT", [128, KC, NT], BF16)
        ar = K.sbuf("ar", [128, 12288], F32)
        arb = ar.t.bitcast(BF16)

        class AV:
            def __init__(self, off, n, fp32=False):
                self.off, self.n, self.fp32 = off, n, fp32

            def ap(self, lo=0, hi=None):
                hi = self.n if hi is None else hi
                if self.fp32:
                    return ar.t[:, self.off // 2 + lo:self.off // 2 + hi]
                return arb[:, self.off + lo:self.off + hi]

            def T(self, lo=0, hi=None):
                hi = self.n if hi is None else hi
                m = 2 if self.fp32 else 1
                a, b_ = self.off + lo * m, self.off + hi * m
                return [ar.T(k) for k in range(a // 1024, (b_ - 1) // 1024 + 1)]

        class _Hid:
            def __getitem__(self, idx):
                return arb[:, 0:JF * NT].rearrange("p (j t) -> p j t", j=JF)[idx]

            def T(self, j, t):
                return ar.T(j)
        hid = _Hid()
        NWB = 2
        WSL = 5632
        wb = K.sbuf("wb", [128, NWB, WSL], BF16)
        wm = K.sbuf("wm", [128, 2, KC, 128], F32)
        sq = K.sbuf("sq", [128, 2, 512], BF16)
        rstd = K.sbuf("rstd", [128, 1, 512], F32)
        tmpf = K.sbuf("tmpf", [128, 3, 512], F32)
        tmpb = K.sbuf("tmpb", [128, 2, 512], BF16)

        class _Stg:
            def __getitem__(self, idx):
                return wm.t[:].rearrange("p s k c -> p s (k c)")[idx]

            def T(self, sl):
                return wm.T(sl)
        stg = _Stg()
        vst = K.sbuf("vst", [128, 128], F32)
        vecA = K.sbuf("vecA", [128, 96], F32)
        vecG = K.sbuf("vecG", [128, 24], F32)
        scond = K.sbuf("scond", [128, KC, 2], F32)
        modv = K.sbuf("modv", [128, 72, 2], F32)
        gsv = K.sbuf("gsv", [128, 3, KC, 2], F32)
        ghv = K.sbuf("ghv", [128, 3, KC, 2], F32)
        identf = K.sbuf("identf", [128, 128], F32)
        identb = K.sbuf("identb", [128, 128], BF16)
        onesb = K.sbuf("onesb", [128, 128], BF16)
        epsb = K.sbuf("epsb", [128, 1], F32)
        ps = K.psum("ps", [128, 8, 512], F32)

        reserved = set()

        def bank():
            while True:
                b = K.nbank % 8
                K.nbank += 1
                if b not in reserved:
                    return b

        K.dma("sp", identf[:], ident_d[:, :], R=[], W=[identf.T()])
        K.cp(identb[:], identf[:], R=[identf.T()], W=[identb.T()])
        K.memset(onesb[:], 1.0, R=[], W=[onesb.T()])
        K.memset(epsb[:], EPS, R=[], W=[epsb.T()])

        ld = 0
        for g, src in ((0, xp), (1, xs)):
            for tb in range(8):
                sl = ld % 2
                ld += 1
                K.dma("sp", stg[:, sl, :], src[tb * 128:(tb + 1) * 128, :], R=[], W=[stg.T(sl)])
                for hf in range(2):
                    b = bank()
                    for q in range(4):
                        kc = hf * 4 + q
                        K.tr(ps[:, b, q * 128:(q + 1) * 128], stg[:, sl, kc * 128:(kc + 1) * 128], identf[:],
                             R=[stg.T(sl), identf.T()], W=[ps.T(b)])
                    tok0 = g * NT + tb * 128
                    K.cp(xT[:, hf * 4:hf * 4 + 4, tok0:tok0 + 128],
                         ps[:, b, :].rearrange("p (q t) -> p q t", q=4),
                         R=[ps.T(b)], W=[xT.T(g, hf * 4 + q, tb // 4) for q in range(4)],
                         eng="act" if hf else "dve")

        K.dma("sp", vst[0:24, :], vecg[:, :], R=[], W=[vst.T()])
        b = bank()
        K.tr(ps[:, b, 0:24], vst[0:24, :], identf[0:24, 0:24], R=[vst.T(), identf.T()], W=[ps.T(b)])
        K.cp(vecG[:], ps[:, b, 0:24], R=[ps.T(b)], W=[vecG.T()])
        K.act(scond[:].rearrange("p k c -> p c k"), vecG[:, 0:16].rearrange("p (c k) -> p c k", c=2), AF.Silu,
              R=[vecG.T()], W=[scond.T()])

        wslot = [0]

        def load_w(dst_views_and_src, tilekey_extra=None):
            s = wslot[0] % NWB
            wslot[0] += 1
            for o, i in dst_views_and_src(s):
                K.dma("pool", o, i, R=[], W=[wb.T(s)])
            return s

        def layer_vectors(l):
            K.dma("sp", vst[0:72, :], b_mod[l], R=[], W=[vst.T()])
            K.dma("sp", vst[72:96, :], norm_g[l], R=[], W=[vst.T()])
            b = bank()
            K.tr(ps[:, b, 0:96], vst[0:96, :], identf[0:96, 0:96], R=[vst.T(), identf.T()], W=[ps.T(b)])
            K.cp(vecA[:], ps[:, b, 0:96], R=[ps.T(b)], W=[vecA.T()])
            wv = w_mod[l].rearrange("(kc p) n -> p kc n", p=128)
            for pc in range(72):
                s = pc % 2
                K.dma("sp", wm[:, s], wv[:, :, pc * 128:(pc + 1) * 128], R=[], W=[wm.T(s)])
                if pc % 8 == 0:
                    b = bank()
                c0 = (pc % 8) * 2
                for kc in range(KC):
                    K.mm(ps[:, b, c0:c0 + 2], wm[:, s, kc, :], scond[:, kc, :],
                         kc == 0, kc == KC - 1, R=[wm.T(s), scond.T()], W=[ps.T(b)])
                if pc % 8 == 7:
                    p0 = pc - 7
                    K.tt(modv[:, p0:p0 + 8, :], ps[:, b, 0:16].rearrange("p (a c) -> p a c", c=2),
                         vecA[:, p0:p0 + 8].unsqueeze(2).broadcast_to([128, 8, 2]), ALU.add,
                         R=[ps.T(b), vecA.T()], W=[modv.T()])
            for s3 in range(3):
                for kc in range(KC):
                    K.ts(gsv[:, s3, kc, :], modv[:, (3 * s3 + 1) * 8 + kc, :], 1.0, vecA[:, 72 + s3 * 8 + kc:72 + s3 * 8 + kc + 1],
                         ALU.add, ALU.mult, R=[modv.T(), vecA.T()], W=[gsv.T()])
                K.ts(ghv[:, s3], modv[:, (3 * s3 + 2) * 8:(3 * s3 + 3) * 8, :], 0.5 if s3 != 1 else 1.0, None, ALU.mult, None,
                     R=[modv.T()], W=[ghv.T()])

        def norm_to(g, t, scale_fn, bias_fn, out_fn, outW):
            tok0 = g * NT + t * 512
            for kc in range(KC):
                pass
            b = bank()
            for kc in range(KC):
                K.act(sq[:, kc % 2, :], xT[:, kc, tok0:tok0 + 512], AF.Square, R=[xT.T(g, kc, t)], W=[sq.T(kc % 2)],
                      eng="act")
                K.mm(ps[:, b, :], onesb[:], sq[:, kc % 2, :], kc == 0, kc == KC - 1, R=[onesb.T(), sq.T(kc % 2)], W=[ps.T(b)])
            rs = 0
            K.act(tmpf[:, 2, :], ps[:, b, :], AF.Sqrt, R=[ps.T(b), epsb.T()], W=[tmpf.T(2)], bias=epsb[:, 0:1], scale=1.0 / D)
            K.recip(rstd[:, rs, :], tmpf[:, 2, :], R=[tmpf.T(2)], W=[rstd.T(rs)])
            for kc in range(KC):
                tb_ = kc % 2
                K.tt(tmpf[:, tb_, :], xT[:, kc, tok0:tok0 + 512], rstd[:, rs, :], ALU.mult,
                     R=[xT.T(g, kc, t), rstd.T(rs)], W=[tmpf.T(tb_)])
                bi = bias_fn(kc)
                K.act(out_fn(kc), tmpf[:, tb_, :], AF.Identity, R=[tmpf.T(tb_), gsv.T(), modv.T(), vecG.T()], W=outW(kc),
                      scale=scale_fn(kc), bias=bi)

        def mod_norm(g, s3):
            for t in range(2):
                norm_to(g, t,
                        lambda kc: gsv[:, s3, kc, g:g + 1],
                        lambda kc: modv[:, (3 * s3) * 8 + kc, g:g + 1],
                        lambda kc: hT[:, kc, t * 512:(t + 1) * 512],
                        lambda kc: [hT.T(kc, t)])

        def ffn(l, s, g):
            s3 = 0 if s == 0 else 2
            mod_norm(g, s3)
            w1 = ffn_w1[l, s].rearrange("(kc p) n -> p kc n", p=128)
            w2 = ffn_w2[l, s].rearrange("(j p) n -> p j n", p=128)
            for p in range(11):
                def views(sl, p=p):
                    v = wb[:, sl, 0:4096].rearrange("p (kc ab c) -> p kc ab c", kc=KC, ab=2)
                    return [(v[:, :, 0, :], w1[:, :, p * 256:(p + 1) * 256]),
                            (v[:, :, 1, :], w1[:, :, DFF + p * 256:DFF + (p + 1) * 256])]
                sl = load_w(views)
                wv = wb[:, sl, 0:4096].rearrange("p (kc ab c) -> p kc ab c", kc=KC, ab=2)
                for t in range(2):
                    for jj in range(2):
                        j = 2 * p + jj
                        ba = bank()
                        for kc in range(KC):
                            K.mm(ps[:, ba, :], wv[:, kc, 0, jj * 128:(jj + 1) * 128], hT[:, kc, t * 512:(t + 1) * 512],
                                 kc == 0, kc == KC - 1, R=[wb.T(sl), hT.T(kc, t)], W=[ps.T(ba)])
                        bb_ = bank()
                        for kc in range(KC):
                            K.mm(ps[:, bb_, :], wv[:, kc, 1, jj * 128:(jj + 1) * 128], hT[:, kc, t * 512:(t + 1) * 512],
                                 kc == 0, kc == KC - 1, R=[wb.T(sl), hT.T(kc, t)], W=[ps.T(bb_)])
                        tb_ = (K.nbank // 2) % 2
                        K.act(tmpb[:, tb_, :], ps[:, ba, :], AF.Silu, R=[ps.T(ba)], W=[tmpb.T(tb_)])
                        K.tt(hid[:, j, t * 512:(t + 1) * 512], tmpb[:, tb_, :], ps[:, bb_, :], ALU.mult,
                             R=[tmpb.T(tb_), ps.T(bb_)], W=[hid.T(j, t)])
            for p2 in range(4):
                def views2(sl, p2=p2):
                    v = wb[:, sl, 0:JF * 256].rearrange("p (j c) -> p j c", j=JF)
                    return [(v, w2[:, :, p2 * 256:(p2 + 1) * 256])]
                sl = load_w(views2)
                wv = wb[:, sl, 0:JF * 256].rearrange("p (j c) -> p j c", j=JF)
                for t in range(2):
                    for bb in range(2):
                        blk = 2 * p2 + bb
                        b = bank()
                        for j in range(JF):
                            K.mm(ps[:, b, :], wv[:, j, bb * 128:(bb + 1) * 128], hid[:, j, t * 512:(t + 1) * 512],
                                 j == 0, j == JF - 1, R=[wb.T(sl), hid.T(j, t)], W=[ps.T(b)])
                        tok0 = g * NT + t * 512
                        K.stt(xT[:, blk, tok0:tok0 + 512], ps[:, b, :], ghv[:, s3, blk, g:g + 1], xT[:, blk, tok0:tok0 + 512],
                              ALU.mult, ALU.add, R=[ps.T(b), ghv.T(), xT.T(g, blk, t)], W=[xT.T(g, blk, t)])

        w_in = din("w_in", [L, D, 5120])
        w_branch = din("w_branch", [L, 3, 512, D])
        w_merge = din("w_merge", [L, D, 3 * D])
        w_out = din("w_out", [L, D, D])
        vecb_d = din("vecb", [L, 40, 128])
        dlog = din("dlog", [L, 128, 8])
        ctab_d = din("ctab_in", [128, NCT])
        nst = dout("nst", [4 * L * 2 * 4 * 128, 128])
        nk = dout("nk", [4 * L * 8 * 256, 64])
        nv = dout("nv", [4 * L * 8 * 256, 64])

        ctab = K.sbuf("ctab", [128, NCT], F32)
        vecB = K.sbuf("vecB", [128, 40], F32)
        dtb = K.sbuf("dtb", [128, 1760], F32)
        cst = K.sbuf("cst", [128, 4], F32)
        sm = K.sbuf("sm", [128, 64], F32)
        atb = K.sbuf("atb", [128, 4, 128], BF16)
        st32 = K.sbuf("st32", [128, 4, 128], F32)
        K.dma("sp", ctab[:], ctab_d[:, :], R=[], W=[ctab.T()])
        K.memset(cst[:, 0:1], EPS, R=[], W=[cst.T()])
        K.memset(cst[:, 1:2], 1.0, R=[], W=[cst.T()])
        K.memset(cst[:, 2:3], 0.0, R=[], W=[cst.T()])
        K.zero_bias = (cst[:, 2:3], cst.T())
        O_LG, O_DM, O_XF, O_XB, O_ZC, O_G, O_BZ, O_CF, O_CB, O_C0, O_TMP = 0, 8, 520, 1032, 1544, 1552, 1560, 1624, 1640, 1656, 1664
        nbb = [0]

        def bankb():
            nbb[0] += 1
            return nbb[0] % 2

        def layer_tables(l):
            DT = [dtb.T()]
            K.dma("sp", vst[0:40, :], vecb_d[l], R=[], W=[vst.T()])
            b = bank()
            K.tr(ps[:, b, 0:40], vst[0:40, :], identf[0:40, 0:40], R=[vst.T(), identf.T()], W=[ps.T(b)])
            K.cp(vecB[:], ps[:, b, 0:40], R=[ps.T(b)], W=[vecB.T()])
            K.dma("sp", dtb[:, O_TMP:O_TMP + 8], dlog[l], R=[], W=DT)
            K.act(dtb[:, O_TMP + 8:O_TMP + 16], dtb[:, O_TMP:O_TMP + 8], AF.Exp, R=DT, W=DT, scale=-1.0)
            K.act(dtb[:, O_TMP + 16:O_TMP + 24], dtb[:, O_TMP + 8:O_TMP + 16], AF.Ln, R=DT + [cst.T()], W=DT, bias=cst[:, 1:2])
            K.ts(dtb[:, O_LG:O_LG + 8], dtb[:, O_TMP + 16:O_TMP + 24], -1.0, None, ALU.mult, None, R=DT, W=DT)
            CT = [ctab.T()]
            for h in range(4):
                lf = dtb[:, O_LG + h:O_LG + h + 1]
                lb = dtb[:, O_LG + 4 + h:O_LG + 5 + h]
                dm = dtb[:, O_DM + h * 128:O_DM + (h + 1) * 128]
                t0 = dtb[:, O_XF + h * 128:O_XF + (h + 1) * 128]
                t1 = dtb[:, O_XB + h * 128:O_XB + (h + 1) * 128]
                K.act(t0, ctab[:, C_RELF:C_RELF + 128], AF.Exp, R=DT + CT, W=DT, scale=lf)
                K.tt(t0, t0, ctab[:, C_MF:C_MF + 128], ALU.mult, R=DT + CT, W=DT)
                K.act(t1, ctab[:, C_RELB:C_RELB + 128], AF.Exp, R=DT + CT, W=DT, scale=lb)
                K.tt(t1, t1, ctab[:, C_MB:C_MB + 128], ALU.mult, R=DT + CT, W=DT)
                K.tt(dm, t0, t1, ALU.add, R=DT, W=DT)
                K.act(t0, ctab[:, C_IO1:C_IO1 + 128], AF.Exp, R=DT + CT, W=DT, scale=lf)
                K.act(t1, ctab[:, C_IOB:C_IOB + 128], AF.Exp, R=DT + CT, W=DT, scale=lb)
                cc = C_COL
                K.act(dtb[:, O_ZC + h:O_ZC + h + 1], ctab[:, cc + 0:cc + 1], AF.Exp, R=DT + CT, W=DT, scale=lf)
                K.act(dtb[:, O_ZC + 4 + h:O_ZC + 5 + h], ctab[:, cc + 1:cc + 2], AF.Exp, R=DT + CT, W=DT, scale=lb)
                K.act(dtb[:, O_G + h:O_G + h + 1], ctab[:, cc + 19:cc + 20], AF.Exp, R=DT + CT, W=DT, scale=lf)
                K.act(dtb[:, O_G + 4 + h:O_G + 5 + h], ctab[:, cc + 19:cc + 20], AF.Exp, R=DT + CT, W=DT, scale=lb)
                K.act(dtb[:, O_BZ + h * 8:O_BZ + h * 8 + 8], ctab[:, cc + 2:cc + 10], AF.Exp, R=DT + CT, W=DT, scale=lf)
                K.act(dtb[:, O_BZ + 32 + h * 8:O_BZ + 40 + h * 8], ctab[:, cc + 10:cc + 18], AF.Exp, R=DT + CT, W=DT, scale=lb)
                K.act(dtb[:, O_CF + h * 4:O_CF + h * 4 + 4], ctab[:, cc + 20:cc + 24], AF.Exp, R=DT + CT, W=DT, scale=lf)
                K.tt(dtb[:, O_CF + h * 4:O_CF + h * 4 + 4], dtb[:, O_CF + h * 4:O_CF + h * 4 + 4], ctab[:, cc + 24:cc + 28], ALU.mult, R=DT + CT, W=DT)
                K.act(dtb[:, O_CB + h * 4:O_CB + h * 4 + 4], ctab[:, cc + 28:cc + 32], AF.Exp, R=DT + CT, W=DT, scale=lb)
                K.tt(dtb[:, O_CB + h * 4:O_CB + h * 4 + 4], dtb[:, O_CB + h * 4:O_CB + h * 4 + 4], ctab[:, cc + 32:cc + 36], ALU.mult, R=DT + CT, W=DT)
                K.act(dtb[:, O_C0 + h:O_C0 + h + 1], ctab[:, cc + 36:cc + 37], AF.Exp, R=DT + CT, W=DT, scale=lf)
                K.act(dtb[:, O_C0 + 4 + h:O_C0 + 5 + h], ctab[:, cc + 37:cc + 38], AF.Exp, R=DT + CT, W=DT, scale=lb)

        def fm_group(wap_fn, t, b):
            for kc in range(KC):
                K.mm(ps[:, b, :], wap_fn(kc), hT[:, kc, t * 512:(t + 1) * 512], kc == 0, kc == KC - 1,
                     R=[wbT[0], hT.T(kc, t)], W=[ps.T(b)])

        wbT = [None]
        DKS = 128.0 ** -0.5
        SC = 64.0 ** -0.5

        def retention(l, g, win, retT):
            DT = [dtb.T()]
            for h in range(4):
                B = 4096 + (h % 2) * 10240
                qrT, krT, kzf, kzb, vr, qxf, qxb, gsil, Sfa, Sba = [AV(B + i * 1024, 1024) for i in range(10)]

                def views(sl, h=h):
                    v = wb[:, sl, 0:4096].rearrange("p (kc s c) -> p kc s c", kc=KC, s=4)
                    return [(v[:, :, si, :], win[:, :, si * 512 + h * 128:si * 512 + h * 128 + 128]) for si in range(4)]
                sl = load_w(views)
                wbT[0] = wb.T(sl)
                wv = wb[:, sl, 0:4096].rearrange("p (kc s c) -> p kc s c", kc=KC, s=4)
                for t in range(2):
                    for si, dst in ((0, qrT), (1, krT), (3, gsil)):
                        b = bank()
                        fm_group(lambda kc, si=si: wv[:, kc, si, :], t, b)
                        o_ap = dst.ap(t * 512, t * 512 + 512)
                        if si == 3:
                            K.act(o_ap, ps[:, b, :], AF.Silu, R=[ps.T(b)], W=dst.T())
                        elif g == 0:
                            K.act(o_ap, ps[:, b, :], AF.Copy, R=[ps.T(b)], W=dst.T(), scale=DKS if si == 0 else 1.0)
                        else:
                            rope_evac(o_ap, dst, b, t, DKS if si == 0 else 1.0)
                for jb in range(2):
                    b = bank()
                    for jj in range(4):
                        j = jb * 4 + jj
                        for kc in range(KC):
                            K.mm(ps[:, b, jj * 128:(jj + 1) * 128], hT[:, kc, j * 128:(j + 1) * 128], wv[:, kc, 2, :],
                                 kc == 0, kc == KC - 1, R=[wb.T(sl), hT.T(kc, j // 4)], W=[ps.T(b)])
                    K.act(vr.ap(jb * 512, jb * 512 + 512), ps[:, b, :], AF.Copy, R=[ps.T(b)], W=vr.T())
                RL = int(os.environ.get("RDBG", 9))
                if RL < 2:
                    continue
                for jb in range(2):
                    pb = bank()
                    for jj in range(4):
                        j = jb * 4 + jj
                        K.mm(ps[:, pb, jj * 128:(jj + 1) * 128], krT.ap(j * 128, (j + 1) * 128), identb[:], True, True,
                             R=krT.T() + [identb.T()], W=[ps.T(pb)])
                    RS = os.environ.get("RSUB", "da")
                    if "d" in RS:
                        K.ts(kzf.ap(jb * 512, jb * 512 + 512), ps[:, pb, 0:512], dtb[:, O_ZC + h:O_ZC + h + 1], None, ALU.mult, None,
                             R=[ps.T(pb)] + DT, W=kzf.T())
                    if "a" in RS:
                        K.act(kzb.ap(jb * 512, jb * 512 + 512), ps[:, pb, 0:512], AF.Identity, R=[ps.T(pb), cst.T()] + DT, W=kzb.T(),
                              scale=dtb[:, O_ZC + 4 + h:O_ZC + 5 + h], bias=cst[:, 2:3])
                if RL < 3:
                    continue
                for dst, o in ((qxf, O_XF), (qxb, O_XB)):
                    for j in range(8):
                        K.tt(dst.ap(j * 128, j * 128 + 128), qrT.ap(j * 128, j * 128 + 128),
                             dtb[:, o + h * 128:o + (h + 1) * 128], ALU.mult, R=qrT.T() + DT, W=dst.T())
                if RL < 4:
                    continue
                has_f, has_b = set(), set()
                gf = dtb[:, O_G + h:O_G + h + 1]
                gb = dtb[:, O_G + 4 + h:O_G + 5 + h]
                if g == 0:
                    for seq in range(4):
                        j0, j1 = 2 * seq, 2 * seq + 1
                        b = bank()
                        for (ja, jb_, kz, gg, Sall, hs, d) in ((j0, j1, kzf, gf, Sfa, has_f, 0), (j1, j0, kzb, gb, Sba, has_b, 1)):
                            c0 = d * 256
                            K.mm(ps[:, b, c0:c0 + 128], kz.ap(ja * 128, ja * 128 + 128), vr.ap(ja * 128, ja * 128 + 128), True, True,
                                 R=kz.T() + vr.T(), W=[ps.T(b)])
                            K.mm(ps[:, b, c0 + 128:c0 + 256], kz.ap(jb_ * 128, jb_ * 128 + 128), vr.ap(jb_ * 128, jb_ * 128 + 128), True, True,
                                 R=kz.T() + vr.T(), W=[ps.T(b)])
                            K.cp(Sall.ap(jb_ * 128, jb_ * 128 + 128), ps[:, b, c0:c0 + 128], R=[ps.T(b)], W=Sall.T(), eng="act")
                            hs.add(jb_)
                            si_ = (seq % 2) * 2 + d
                            K.cp(st32[:, si_, :], ps[:, b, c0 + 128:c0 + 256], R=[ps.T(b)], W=[st32.T(si_)])
                            K.stt(st32[:, si_, :], ps[:, b, c0:c0 + 128], gg, st32[:, si_, :], ALU.mult, ALU.add,
                                  R=[ps.T(b), st32.T(si_)] + DT, W=[st32.T(si_)])
                            r0_ = (((seq * L + l) * 2 + d) * 4 + h) * 128
                            K.dma("sp", nst[r0_:r0_ + 128, :], st32[:, si_, :], R=[st32.T(si_)], W=[nst.T(seq, l, d, h)])
                else:
                    sample_scan(l, h, kzf, kzb, vr, Sfa, Sba, has_f, has_b, gf, gb)
                if RL < 5:
                    continue
                for j in range(8):
                    r = j % 2
                    b = bank()
                    K.mm(ps[:, b, 0:128], krT.ap(j * 128, j * 128 + 128), qrT.ap(j * 128, j * 128 + 128), True, True,
                         R=krT.T() + qrT.T(), W=[ps.T(b)])
                    K.tt(atb[:, r, :], ps[:, b, 0:128], dtb[:, O_DM + h * 128:O_DM + (h + 1) * 128], ALU.mult,
                         R=[ps.T(b)] + DT, W=[atb.T(r)])
                    nmm = 1 + (j in has_f) + (j in has_b)
                    k_ = 0
                    K.mm(ps[:, b, 128:256], atb[:, r, :], vr.ap(j * 128, j * 128 + 128), True, nmm == 1,
                         R=[atb.T(r)] + vr.T(), W=[ps.T(b)])
                    for (hs, qx, Sall) in ((has_f, qxf, Sfa), (has_b, qxb, Sba)):
                        if j in hs:
                            k_ += 1
                            K.mm(ps[:, b, 128:256], qx.ap(j * 128, j * 128 + 128), Sall.ap(j * 128, j * 128 + 128), False, k_ == nmm - 1,
                                 R=qx.T() + Sall.T(), W=[ps.T(b)])
                    o_ps = ps[:, b, 128:256]
                    s0 = r * 16
                    SM = [sm.T(r)]
                    K.emit("dve", lambda h_, o_ps=o_ps, s0=s0: h_.bn_stats(sm[:, s0:s0 + 6], o_ps), R=[ps.T(b)], W=SM)
                    K.emit("dve", lambda h_, s0=s0: h_.bn_aggr(sm[:, s0 + 6:s0 + 8], sm[:, s0:s0 + 6]), R=SM, W=SM)
                    K.act(sm[:, s0 + 8:s0 + 9], sm[:, s0 + 7:s0 + 8], AF.Sqrt, R=SM + [cst.T()], W=SM, bias=cst[:, 0:1], scale=1.0)
                    K.recip(sm[:, s0 + 9:s0 + 10], sm[:, s0 + 8:s0 + 9], R=SM, W=SM)
                    K.ts(atb[:, 2 + r, :], o_ps, sm[:, s0 + 6:s0 + 7], sm[:, s0 + 9:s0 + 10], ALU.subtract, ALU.mult,
                         R=[ps.T(b)] + SM, W=[atb.T(2 + r)])
                    pb = bank()
                    K.mm(ps[:, pb, 0:128], atb[:, 2 + r, :], identb[:], True, True, R=[atb.T(2 + r), identb.T()], W=[ps.T(pb)])
                    K.tt(retT.ap(h * 1024 + j * 128, h * 1024 + j * 128 + 128), ps[:, pb, 0:128], gsil.ap(j * 128, j * 128 + 128), ALU.mult,
                         R=[ps.T(pb)] + gsil.T(), W=retT.T(h * 1024 + j * 128, h * 1024 + j * 128 + 128))

        def conv(l, g, win, convT):
            zbuf = AV(12288, 1032, fp32=True)
            cbt = AV(12288 + 2064, 1024)
            for c in range(4):
                def views(sl, c=c):
                    v = wb[:, sl, 0:3072].rearrange("p (kc s c) -> p kc s c", kc=KC, s=3)
                    return [(v[:, :, si, :], win[:, :, (4 + si) * 512 + c * 128:(4 + si) * 512 + c * 128 + 128]) for si in range(3)]
                sl = load_w(views)
                wbT[0] = wb.T(sl)
                wv = wb[:, sl, 0:3072].rearrange("p (kc s c) -> p kc s c", kc=KC, s=3)
                if g == 0:
                    zv = zbuf.ap(0, 1032).rearrange("p (s c) -> p s c", s=4)
                    K.memset(zv[:, :, 0:258:257], 0.0, R=[], W=zbuf.T())
                else:
                    sample_conv_pads(zbuf, c)
                for t in range(2):
                    bs = []
                    for si in range(3):
                        b = bank()
                        bs.append(b)
                        fm_group(lambda kc, si=si: wv[:, kc, si, :], t, b)
                    K.act(cbt.ap(t * 512, t * 512 + 512), ps[:, bs[0], :], AF.Copy, R=[ps.T(bs[0])], W=cbt.T())
                    K.act(tmpf[:, 0, :], ps[:, bs[2], :], AF.Copy, R=[ps.T(bs[2])], W=[tmpf.T(0)])
                    if g == 0:
                        K.tt(zv[:, 2 * t:2 * t + 2, 1:257], tmpf[:, 0, :].rearrange("p (s c) -> p s c", s=2),
                             ps[:, bs[1], :].rearrange("p (s c) -> p s c", s=2), ALU.mult,
                             R=[tmpf.T(0), ps.T(bs[1])], W=zbuf.T())
                    else:
                        K.tt(zbuf.ap(1 + t * 512, 1 + t * 512 + 512), tmpf[:, 0, :], ps[:, bs[1], :], ALU.mult,
                             R=[tmpf.T(0), ps.T(bs[1])], W=zbuf.T())
                for t in range(2):
                    if g == 0:
                        zc_ = zv[:, 2 * t:2 * t + 2, 1:257]
                        zl = zv[:, 2 * t:2 * t + 2, 0:256]
                        zr = zv[:, 2 * t:2 * t + 2, 2:258]
                        y = tmpf[:, 1, :].rearrange("p (s c) -> p s c", s=2)
                    else:
                        zc_ = zbuf.ap(1 + t * 512, 513 + t * 512)
                        zl = zbuf.ap(t * 512, 512 + t * 512)
                        zr = zbuf.ap(2 + t * 512, 514 + t * 512)
                        y = tmpf[:, 1, :]
                    VB = [vecB.T()]
                    K.ts(y, zc_, vecB[:, 24 + 4 + c:24 + 5 + c], vecB[:, 36 + c:37 + c], ALU.mult, ALU.add,
                         R=zbuf.T() + VB, W=[tmpf.T(1)])
                    K.stt(y, zl, vecB[:, 24 + c:25 + c], y, ALU.mult, ALU.add, R=zbuf.T() + VB + [tmpf.T(1)], W=[tmpf.T(1)])
                    K.stt(y, zr, vecB[:, 24 + 8 + c:24 + 9 + c], y, ALU.mult, ALU.add, R=zbuf.T() + VB + [tmpf.T(1)], W=[tmpf.T(1)])
                    K.tt(convT.ap(c * 1024 + t * 512, c * 1024 + t * 512 + 512), tmpf[:, 1, :], cbt.ap(t * 512, t * 512 + 512), ALU.mult,
                         R=[tmpf.T(1)] + cbt.T(), W=convT.T(c * 1024 + t * 512, c * 1024 + t * 512 + 512))

        def attention(l, g, win, naT):
            for hh in range(2):
                qnT = AV(12288, 2048)
                knT = AV(14336, 3072)
                vn = AV(17408, 3072)
                Pb = AV(20480, 1024)
                PTb = AV(21504, 1024)
                osb = AV(22528, 256)

                def viewsA(sl, hh=hh):
                    v = wb[:, sl, 0:4096].rearrange("p (kc s c) -> p kc s c", kc=KC, s=2)
                    return [(v[:, :, si, :], win[:, :, (7 + si) * 512 + hh * 256:(7 + si) * 512 + hh * 256 + 256]) for si in range(2)]
                slA = load_w(viewsA)
                wA = wb[:, slA, 0:4096].rearrange("p (kc s c) -> p kc s c", kc=KC, s=2)

                def viewsB(sl, hh=hh):
                    v = wb[:, sl, 0:2048].rearrange("p (kc c) -> p kc c", kc=KC)
                    return [(v, win[:, :, 9 * 512 + hh * 256:9 * 512 + hh * 256 + 256])]
                slB = load_w(viewsB)
                wB = wb[:, slB, 0:2048].rearrange("p (kc c) -> p kc c", kc=KC)
                wbT[0] = wb.T(slA)
                koff = 0 if g == 0 else 256
                kw = 1024 if g == 0 else 1536
                n = 0
                for t in range(2):
                    for cc in range(2):
                        for si, dst, off in ((0, qnT, cc * 1024 + t * 512), (1, knT, cc * kw + koff + t * 512)):
                            b = bank()
                            fm_group(lambda kc, si=si, cc=cc: wA[:, kc, si, cc * 128:(cc + 1) * 128], t, b)
                            K.cp(dst.ap(off, off + 512), ps[:, b, :], R=[ps.T(b)], W=dst.T(off, off + 512), eng="act" if n % 2 else "dve")
                            n += 1
                for j in range(8):
                    b = bank()
                    r = 2
                    if g == 0:
                        for kc in range(KC):
                            K.mm(ps[:, b, 0:256], hT[:, kc, j * 128:(j + 1) * 128], wA[:, kc, 1, :], kc == 0, kc == KC - 1,
                                 R=[wb.T(slA), hT.T(kc, j // 4)], W=[ps.T(b)])
                    for kc in range(KC):
                        K.mm(ps[:, b, 256:512], hT[:, kc, j * 128:(j + 1) * 128], wB[:, kc, :], kc == 0, kc == KC - 1,
                             R=[wb.T(slB), hT.T(kc, j // 4)], W=[ps.T(b)])
                    vo = (j if g == 0 else j + 2) * 256
                    K.cp(vn.ap(vo, vo + 256), ps[:, b, 256:512], R=[ps.T(b)], W=vn.T(vo, vo + 256), eng="act")
                    if g == 0:
                        K.cp(tmpf[:, r, :], ps[:, b, :], R=[ps.T(b)], W=[tmpf.T(r)])
                        seq, t0_ = j // 2, (j % 2) * 128
                        for dd, o in ((nk, 0), (nv, 256)):
                            for h4 in range(4):
                                r0_ = ((seq * L + l) * 8 + 4 * hh + h4) * 256 + t0_
                                K.dma("sp", dd[r0_:r0_ + 128, :], tmpf[:, r, o + h4 * 64:o + h4 * 64 + 64],
                                      R=[tmpf.T(r)], W=[dd.T(seq, l, hh, j)])
                if g == 1:
                    sample_na_prep(l, hh, knT, vn)
                    K.memset(arb[64:128, 22784:22912], 0.0, R=[], W=AV(22784, 128).T())
                    K.memset(arb[0:64, 22912:23040], 0.0, R=[], W=AV(22912, 128).T())
                AL = int(os.environ.get("ADBG", 9))
                if AL < 2:
                    continue
                for j in range(8):
                    reserved.clear()
                    bo = bank()
                    reserved.add(bo)
                    SMT = [sm.T(2)]
                    if g == 0:
                        seq = j // 2
                        sb = []
                        for hl in range(4):
                            cc, pr = hl // 2, (hl % 2) * 64
                            qap = arb[pr:pr + 64, qnT.off + cc * 1024 + j * 128:qnT.off + cc * 1024 + j * 128 + 128]
                            kap = arb[pr:pr + 64, knT.off + cc * 1024 + seq * 256:knT.off + cc * 1024 + seq * 256 + 256]
                            b = bank()
                            sb.append(b)
                            K.mm(ps[:, b, 0:256], qap, kap, True, True, R=qnT.T() + knT.T(), W=[ps.T(b)])
                        for hl in range(4):
                            mc = 32 + hl * 4
                            K.emit("dve", lambda h_, b=sb[hl], mc=mc: h_.reduce_max(sm[:, mc:mc + 1], ps[:, b, 0:256], mybir.AxisListType.X),
                                   R=[ps.T(sb[hl])], W=SMT)
                        for hl in range(4):
                            mc = 32 + hl * 4
                            K.ts(sm[:, mc + 2:mc + 3], sm[:, mc:mc + 1], -SC, None, ALU.mult, None, R=SMT, W=SMT)
                        for hl in range(4):
                            mc = 32 + hl * 4
                            K.act(Pb.ap(hl * 256, hl * 256 + 256), ps[:, sb[hl], 0:256], AF.Exp, R=[ps.T(sb[hl])] + SMT, W=Pb.T() + SMT,
                                  scale=SC, bias=sm[:, mc + 2:mc + 3], accum_out=sm[:, 48 + hl * 2:48 + hl * 2 + 1])
                        for hl in range(4):
                            pb = bank()
                            for kb in range(2):
                                K.mm(ps[:, pb, kb * 128:(kb + 1) * 128], Pb.ap(hl * 256 + kb * 128, hl * 256 + (kb + 1) * 128), identb[:], True, True,
                                     R=Pb.T() + [identb.T()], W=[ps.T(pb)])
                            K.cp(PTb.ap(hl * 256, hl * 256 + 256), ps[:, pb, 0:256], R=[ps.T(pb)], W=PTb.T(), eng="dve" if hl % 2 else "act")
                        for hl in range(4):
                            for kb in range(2):
                                vo = (2 * seq + kb) * 256 + hl * 64
                                K.mm(ps[:, bo, hl * 64:(hl + 1) * 64], PTb.ap(hl * 256 + kb * 128, hl * 256 + (kb + 1) * 128), vn.ap(vo, vo + 64),
                                     kb == 0, kb == 1, R=PTb.T() + vn.T(), W=[ps.T(bo)])
                    for hl in (range(4) if g == 1 else ()):
                        cc, pr = hl // 2, (hl % 2) * 64
                        qap = arb[pr:pr + 64, qnT.off + cc * 1024 + j * 128:qnT.off + cc * 1024 + j * 128 + 128]
                        if g == 0:
                            seq = j // 2
                            b = bank()
                            kap = arb[pr:pr + 64, knT.off + cc * 1024 + seq * 256:knT.off + cc * 1024 + seq * 256 + 256]
                            K.mm(ps[:, b, 0:256], qap, kap, True, True, R=qnT.T() + knT.T(), W=[ps.T(b)])
                            segs = [(b, 0, 256)]
                            vsrc = [(vn, (2 * seq + kb) * 256 + hl * 64) for kb in range(2)]
                        else:
                            segs, vsrc = sample_scores(l, hh, j, hl, cc, pr, qap, knT, vn, qnT)
                        nseg = len(segs)
                        mcol = 32 + hl * 4
                        for i_, (b, c0, c1) in enumerate(segs):
                            K.emit("dve", lambda h_, b=b, c0=c0, c1=c1, mc=mcol + i_: h_.reduce_max(sm[:, mc:mc + 1], ps[:, b, c0:c1], mybir.AxisListType.X),
                                   R=[ps.T(b)], W=SMT)
                        if AL < 3:
                            continue
                        if nseg == 2:
                            K.tt(sm[:, mcol:mcol + 1], sm[:, mcol:mcol + 1], sm[:, mcol + 1:mcol + 2], ALU.max, R=SMT, W=SMT)
                        K.ts(sm[:, mcol + 2:mcol + 3], sm[:, mcol:mcol + 1], -SC, None, ALU.mult, None, R=SMT, W=SMT)
                        po = 0
                        for i_, (b, c0, c1) in enumerate(segs):
                            K.act(Pb.ap(po, po + c1 - c0), ps[:, b, c0:c1], AF.Exp, R=[ps.T(b)] + SMT, W=Pb.T() + SMT,
                                  scale=SC, bias=sm[:, mcol + 2:mcol + 3], accum_out=sm[:, 48 + hl * 2 + i_:48 + hl * 2 + i_ + 1])
                            po += c1 - c0
                        if nseg == 2:
                            K.tt(sm[:, 48 + hl * 2:48 + hl * 2 + 1], sm[:, 48 + hl * 2:48 + hl * 2 + 1], sm[:, 48 + hl * 2 + 1:48 + hl * 2 + 2], ALU.add, R=SMT, W=SMT)
                        if AL < 4:
                            continue
                        nkb = po // 128
                        for k0 in range(0, nkb, 4):
                            pb = bank()
                            k1 = min(nkb, k0 + 4)
                            for kb in range(k0, k1):
                                K.mm(ps[:, pb, (kb - k0) * 128:(kb - k0 + 1) * 128], Pb.ap(kb * 128, (kb + 1) * 128), identb[:], True, True,
                                     R=Pb.T() + [identb.T()], W=[ps.T(pb)])
                            K.cp(PTb.ap(k0 * 128, k1 * 128), ps[:, pb, 0:(k1 - k0) * 128], R=[ps.T(pb)], W=PTb.T(),
                                 eng="dve" if (hl + k0 // 4) % 2 else "act")
                        for kb in range(nkb):
                            vb, vo = vsrc[kb]
                            K.mm(ps[:, bo, hl * 64:(hl + 1) * 64], PTb.ap(kb * 128, (kb + 1) * 128), vb.ap(vo, vo + 64) if vb is not None else vo,
                                 kb == 0, kb == nkb - 1, R=PTb.T() + (vb.T() if vb is not None else [cvb.T()]), W=[ps.T(bo)])
                    reserved.clear()
                    if AL < 5:
                        continue
                    K.recip(sm[:, 56:60], sm[:, 48:56:2], R=SMT, W=SMT)
                    K.tt(osb.ap().rearrange("p (h d) -> p h d", h=4), ps[:, bo, 0:256].rearrange("p (h d) -> p h d", h=4),
                         sm[:, 56:60].unsqueeze(2).broadcast_to([128, 4, 64]), ALU.mult, R=[ps.T(bo)] + SMT, W=osb.T())
                    pb = bank()
                    for cc in range(2):
                        K.mm(ps[:, pb, cc * 128:(cc + 1) * 128], osb.ap(cc * 128, cc * 128 + 128), identb[:], True, True,
                             R=osb.T() + [identb.T()], W=[ps.T(pb)])
                    for cc in range(2):
                        o_ = (2 * hh + cc) * 1024 + j * 128
                        K.cp(naT.ap(o_, o_ + 128), ps[:, pb, cc * 128:(cc + 1) * 128], R=[ps.T(pb)], W=naT.T(o_, o_ + 128),
                             eng="act" if cc else "dve")

        def merge(l, g, retT, convT, naT):
            merged = AV(12288, 8192)
            brs = (retT, convT, naT)
            wmv = w_merge[l].rearrange("(kc p) (b n) -> p kc b n", p=128, b=3)
            for blk in range(8):
                def views(sl, blk=blk):
                    v = wb[:, sl, 0:3072].rearrange("p (kc b c) -> p kc b c", kc=KC, b=3)
                    v2 = wb[:, sl, 3072:4608].rearrange("p (b k c) -> p b k c", b=3, k=4)
                    r_ = [(v[:, :, b3, :], wmv[:, :, b3, blk * 128:(blk + 1) * 128]) for b3 in range(3)]
                    r_ += [(v2[:, b3], w_branch[l, b3].rearrange("(k p) n -> p k n", p=128)[:, :, blk * 128:(blk + 1) * 128]) for b3 in range(3)]
                    return r_
                sl = load_w(views)
                wbT[0] = wb.T(sl)
                wg = wb[:, sl, 0:3072].rearrange("p (kc b c) -> p kc b c", kc=KC, b=3)
                wr = wb[:, sl, 3072:4608].rearrange("p (b k c) -> p b k c", b=3, k=4)
                for t in range(2):
                    for b3 in range(3):
                        bg = bank()
                        fm_group(lambda kc, b3=b3: wg[:, kc, b3, :], t, bg)
                        r = b3 % 2
                        K.act(tmpb[:, r, :], ps[:, bg, :], AF.Sigmoid, R=[ps.T(bg), vecB.T()], W=[tmpb.T(r)],
                              bias=vecB[:, b3 * 8 + blk:b3 * 8 + blk + 1])
                        bw = bank()
                        for k4 in range(4):
                            K.mm(ps[:, bw, :], wr[:, b3, k4, :], brs[b3].ap(k4 * 1024 + t * 512, k4 * 1024 + t * 512 + 512), k4 == 0, k4 == 3,
                                 R=[wb.T(sl)] + brs[b3].T(k4 * 1024 + t * 512, k4 * 1024 + t * 512 + 512), W=[ps.T(bw)])
                        mo = blk * 1024 + t * 512
                        if b3 == 0:
                            K.tt(tmpf[:, 0, :], tmpb[:, r, :], ps[:, bw, :], ALU.mult, R=[tmpb.T(r), ps.T(bw)], W=[tmpf.T(0)])
                        else:
                            K.tt(tmpf[:, 1, :], tmpb[:, r, :], ps[:, bw, :], ALU.mult, R=[tmpb.T(r), ps.T(bw)], W=[tmpf.T(1)])
                            if b3 == 1:
                                K.tt(tmpf[:, 0, :], tmpf[:, 0, :], tmpf[:, 1, :], ALU.add, R=[tmpf.T(0), tmpf.T(1)], W=[tmpf.T(0)])
                            else:
                                K.tt(merged.ap(mo, mo + 512), tmpf[:, 0, :], tmpf[:, 1, :], ALU.add, R=[tmpf.T(0), tmpf.T(1)],
                                     W=merged.T(mo, mo + 512))
            wo = w_out[l].rearrange("(kc p) n -> p kc n", p=128)
            for p4 in range(4):
                def views(sl, p4=p4):
                    v = wb[:, sl, 0:2048].rearrange("p (kc c) -> p kc c", kc=KC)
                    return [(v, wo[:, :, p4 * 256:(p4 + 1) * 256])]
                sl = load_w(views)
                wv = wb[:, sl, 0:2048].rearrange("p (kc c) -> p kc c", kc=KC)
                for t in range(2):
                    for bb in range(2):
                        blk = p4 * 2 + bb
                        b = bank()
                        for kc in range(KC):
                            mo = kc * 1024 + t * 512
                            K.mm(ps[:, b, :], wv[:, kc, bb * 128:(bb + 1) * 128], merged.ap(mo, mo + 512), kc == 0, kc == KC - 1,
                                 R=[wb.T(sl)] + merged.T(mo, mo + 512), W=[ps.T(b)])
                        tok0 = g * NT + t * 512
                        K.stt(xT[:, blk, tok0:tok0 + 512], ps[:, b, :], ghv[:, 1, blk, g:g + 1], xT[:, blk, tok0:tok0 + 512],
                              ALU.mult, ALU.add, R=[ps.T(b), ghv.T(), xT.T(g, blk, t)], W=[xT.T(g, blk, t)])

        if os.environ.get("KDUMP"):
            dbg = Buf(nc.dram_tensor("dbg", [128, 12288], BF16, kind="ExternalOutput").ap())

        def ar_all():
            return [ar.T(k) for k in range(24)]

        def mixer(l, g):
            if g == 1:
                sample_halo_prep(l)
            mod_norm(g, 1)
            win = w_in[l].rearrange("(kc p) n -> p kc n", p=128)
            retT, convT, naT = AV(0, 4096), AV(4096, 4096), AV(8192, 4096)
            DBG_ = DBG if g == 0 else os.environ.get("KDBGS", "rcam")
            if "r" in DBG_:
                retention(l, g, win, retT)
            if "c" in DBG_:
                conv(l, g, win, convT)
            if "a" in DBG_:
                attention(l, g, win, naT)
            if os.environ.get("KDUMP") and l == 0 and g == int(os.environ.get("KDUMP")):
                K.dma("sp", dbg[:, :], arb[:, 0:12288], R=ar_all(), W=[dbg.T()])
            if "m" in DBG_:
                merge(l, g, retT, convT, naT)

        st0 = din("st0", [L, 2, 4, 128, 128])
        ck_d = din("ck", [L, 8, 256, 64])
        cv_d = din("cv", [L, 8, 256, 64])
        ropet_d = din("ropet_in", [128, 2048])
        natab_d = din("natab_in", [128, 1792])
        rpb_d = din("rpb", [L, 8, 15, 31])
        jrev_d = din("jrev_in", [128, 128])
        padinit = din("padinit", [L, 8, 2176])
        padd_h = nc.dram_tensor("padd", [L, 8, 2176], F32, kind="Internal")
        padd = Buf(padd_h.ap())
        CCW = (2048, 2048, CW - 4096)
        ccsB = [Buf(nc.dram_tensor("cc_src%d" % i, [128, CCW[i]], F32, kind="Internal").ap()) for i in range(3)]
        ccdB = [Buf(nc.dram_tensor("cc_dst%d" % i, [512, CCW[i]], F32, kind="Internal").ap()) for i in range(3)]
        ccsT = [b_.T() for b_ in ccsB]
        ccdT = [b_.T() for b_ in ccdB]

        def ccs_ap(c0, c1):
            i = min(c0 // 2048, 2)
            return ccsB[i][:, c0 - 2048 * i:c1 - 2048 * i]

        def ccv_ap(c0, c1):
            i = min(c0 // 2048, 2)
            return ccdB[i][:, :].rearrange("(r p) c -> p r c", p=128)[:, :, c0 - 2048 * i:c1 - 2048 * i]
        ropet = K.sbuf("ropet", [128, 2048], BF16)
        natab = K.sbuf("natab", [128, 1792], BF16)
        jrev = K.sbuf("jrev", [128, 128], BF16)
        Ub = K.sbuf("Ub", [128, 4096], BF16)
        ckT = K.sbuf("ckT", [128, 4, 256], BF16)
        cvb = K.sbuf("cvb", [128, 2, 512], BF16)
        zh = K.sbuf("zh", [128, 48], F32)
        K.dma("pool", ropet[:], ropet_d[:, :], R=[], W=[ropet.T()])
        K.dma("pool", natab[:], natab_d[:, :], R=[], W=[natab.T()])
        K.dma("pool", jrev[:], jrev_d[:, :], R=[], W=[jrev.T()])
        K.dma("sp", padd[:, :, :], padinit[:, :, :], R=[], W=[padd.T()])
        for l_ in range(L):
            K.dma("sp", padd[l_].rearrange("h (r x) -> h r x", r=17)[:, 1:16, 48:79], rpb_d[l_], R=[], W=[padd.T()])
        GROUPS = [[0, 1, 2, 3], [4, 5, 6, 7]]

        def rope_evac(o_ap, dst, b, t, scale):
            K.act(tmpf[:, 0, :], ps[:, b, :], AF.Copy, R=[ps.T(b)], W=[tmpf.T(0)], scale=scale)
            b2 = bank()
            K.mm(ps[:, b2, :], ctab[:, C_SW:C_SW + 128], tmpf[:, 0, :], True, True, R=[ctab.T(), tmpf.T(0)], W=[ps.T(b2)])
            K.tt(tmpf[:, 1, :], tmpf[:, 0, :], ropet[:, t * 512:(t + 1) * 512], ALU.mult, R=[tmpf.T(0), ropet.T()], W=[tmpf.T(1)])
            K.tt(tmpf[:, 2, :], ps[:, b2, :], ropet[:, 1024 + t * 512:1024 + (t + 1) * 512], ALU.mult,
                 R=[ps.T(b2), ropet.T()], W=[tmpf.T(2)])
            K.tt(o_ap, tmpf[:, 1, :], tmpf[:, 2, :], ALU.add, R=[tmpf.T(1), tmpf.T(2)], W=dst.T())

        def v_tokmajor(vr, wv_fn, sl):
            for jb in range(2):
                b = bank()
                for jj in range(4):
                    j = jb * 4 + jj
                    for kc in range(KC):
                        K.mm(ps[:, b, jj * 128:(jj + 1) * 128], hT[:, kc, j * 128:(j + 1) * 128], wv_fn(kc),
                             kc == 0, kc == KC - 1, R=[wb.T(sl), hT.T(kc, j // 4)], W=[ps.T(b)])
                K.act(vr.ap(jb * 512, jb * 512 + 512), ps[:, b, :], AF.Copy, R=[ps.T(b)], W=vr.T())

        def sample_pre(l):
            DT = [dtb.T()]
            win = w_in[l].rearrange("(kc p) n -> p kc n", p=128)
            for h in range(4):
                B = 4096 + (h % 2) * 10240
                krT, kZf, kZb, vr = AV(B + 1024, 1024), AV(B + 2048, 1024), AV(B + 3072, 1024), AV(B + 4096, 1024)

                def views(sl, h=h):
                    v = wb[:, sl, 0:2048].rearrange("p (kc s c) -> p kc s c", kc=KC, s=2)
                    return [(v[:, :, si, :], win[:, :, (1 + si) * 512 + h * 128:(1 + si) * 512 + h * 128 + 128]) for si in range(2)]
                sl = load_w(views)
                wbT[0] = wb.T(sl)
                wv = wb[:, sl, 0:2048].rearrange("p (kc s c) -> p kc s c", kc=KC, s=2)
                for t in range(2):
                    b = bank()
                    fm_group(lambda kc: wv[:, kc, 0, :], t, b)
                    rope_evac(krT.ap(t * 512, t * 512 + 512), krT, b, t, 1.0)
                v_tokmajor(vr, lambda kc: wv[:, kc, 1, :], sl)
                for jb in range(2):
                    pb = bank()
                    for jj in range(4):
                        j = jb * 4 + jj
                        K.mm(ps[:, pb, jj * 128:(jj + 1) * 128], krT.ap(j * 128, (j + 1) * 128), identb[:], True, True,
                             R=krT.T() + [identb.T()], W=[ps.T(pb)])
                    for kz, o in ((kZf, O_BZ + h * 8 + jb * 4), (kZb, O_BZ + 32 + h * 8 + jb * 4)):
                        K.tt(kz.ap(jb * 512, jb * 512 + 512).rearrange("p (j c) -> p j c", j=4),
                             ps[:, pb, :].rearrange("p (j c) -> p j c", j=4),
                             dtb[:, o:o + 4].unsqueeze(2).broadcast_to([128, 4, 128]), ALU.mult,
                             R=[ps.T(pb)] + DT, W=kz.T())
                b = bank()
                for d, kz in ((0, kZf), (1, kZb)):
                    for j in range(8):
                        K.mm(ps[:, b, d * 128:(d + 1) * 128], kz.ap(j * 128, j * 128 + 128), vr.ap(j * 128, j * 128 + 128), j == 0, j == 7,
                             R=kz.T() + vr.T(), W=[ps.T(b)])
                K.cp(tmpf[:, 2, 0:256], ps[:, b, 0:256], R=[ps.T(b)], W=[tmpf.T(2)])
                for d in range(2):
                    K.dma("sp", ccs_ap((d * 4 + h) * 128, (d * 4 + h) * 128 + 128), tmpf[:, 2, d * 128:(d + 1) * 128],
                          R=[tmpf.T(2)], W=ccsT)
            for si_, base in ((8, 1024), (9, 3072)):
                def views(sl, si_=si_):
                    v = wb[:, sl, 0:4096].rearrange("p (kc c) -> p kc c", kc=KC)
                    return [(v, win[:, :, si_ * 512:(si_ + 1) * 512])]
                sl = load_w(views)
                wv = wb[:, sl, 0:4096].rearrange("p (kc c) -> p kc c", kc=KC)
                for c in range(4):
                    b = bank()
                    if si_ == 8:
                        for ri, tok0 in enumerate((0, 768)):
                            for kc in range(KC):
                                K.mm(ps[:, b, ri * 256:(ri + 1) * 256], wv[:, kc, c * 128:(c + 1) * 128], hT[:, kc, tok0:tok0 + 256],
                                     kc == 0, kc == KC - 1, R=[wb.T(sl), hT.T(kc, ri)], W=[ps.T(b)])
                    else:
                        j = (0, 1, 6, 7)[c]
                        for kc in range(KC):
                            K.mm(ps[:, b, :], hT[:, kc, j * 128:(j + 1) * 128], wv[:, kc, :], kc == 0, kc == KC - 1,
                                 R=[wb.T(sl), hT.T(kc, j // 4)], W=[ps.T(b)])
                    K.cp(tmpf[:, c % 2, :], ps[:, b, :], R=[ps.T(b)], W=[tmpf.T(c % 2)], eng="act" if c % 2 else "dve")
                    K.dma("sp", ccs_ap(base + c * 512, base + (c + 1) * 512), tmpf[:, c % 2, :], R=[tmpf.T(c % 2)], W=ccsT)
            sls = []
            for si_ in (5, 6):
                def views(sl, si_=si_):
                    v = wb[:, sl, 0:4096].rearrange("p (kc c) -> p kc c", kc=KC)
                    return [(v, win[:, :, si_ * 512:(si_ + 1) * 512])]
                sls.append(load_w(views))
            b = bank()
            for c in range(4):
                for q_, sl in enumerate(sls):
                    wv = wb[:, sl, 0:4096].rearrange("p (kc c) -> p kc c", kc=KC)
                    for kc in range(KC):
                        K.mm(ps[:, b, c * 4 + q_ * 2:c * 4 + q_ * 2 + 2], wv[:, kc, c * 128:(c + 1) * 128], hT[:, kc, 0:1024:1023],
                             kc == 0, kc == KC - 1, R=[wb.T(sl), hT.T(kc, 0), hT.T(kc, 1)], W=[ps.T(b)])
            K.cp(sm[:, 0:16], ps[:, b, 0:16], R=[ps.T(b)], W=[sm.T(0)])
            smv = sm[:, 0:16].rearrange("p (c s) -> p c s", c=4)
            K.tt(tmpf[:, 2, 256:264].rearrange("p (c s) -> p c s", c=4), smv[:, :, 0:2], smv[:, :, 2:4], ALU.mult,
                 R=[sm.T(0)], W=[tmpf.T(2)])
            K.dma("sp", ccs_ap(5120, 5128), tmpf[:, 2, 256:264], R=[tmpf.T(2)], W=ccsT)
            for i_ in range(3):
                K.collective(ccsB[i_][:, :], ccdB[i_][:, :], GROUPS, R=[ccsT[i_]], W=[ccdT[i_]],
                             flag_ap=zh[:, 40 + i_:41 + i_], flag_tile=zh.T("flag"))

        def sample_scan(l, h, kzf, kzb, vr, Sfa, Sba, has_f, has_b, gf, gb):
            DT = [dtb.T()]
            for d, (kz, gg, Sall, hs, order) in enumerate(((kzf, gf, Sfa, has_f, list(range(8))),
                                                           (kzb, gb, Sba, has_b, list(range(7, -1, -1))))):
                sl4 = tmpf[:, d, :].rearrange("p (r c) -> p r c", r=4)
                K.dma("sp", sl4, ccv_ap((d * 4 + h) * 128, (d * 4 + h) * 128 + 128), R=ccdT, W=[tmpf.T(d)])
                K.dma("sp", st32[:, d, :], st0[l, d, h], R=[], W=[st32.T(d)])
                S = st32[:, 2 + d, :]
                ST = [st32.T(2 + d)]
                K.ts(S, st32[:, d, :], dtb[:, O_C0 + d * 4 + h:O_C0 + d * 4 + h + 1], None, ALU.mult, None, R=[st32.T(d)] + DT, W=ST)
                cofs = O_CF if d == 0 else O_CB
                for i in range(4):
                    K.stt(S, sl4[:, i, :], dtb[:, cofs + h * 4 + i:cofs + h * 4 + i + 1], S, ALU.mult, ALU.add,
                          R=[tmpf.T(d)] + ST + DT, W=ST)
                bks = [bank(), bank()]
                for j in range(8):
                    K.mm(ps[:, bks[j // 4], (j % 4) * 128:(j % 4 + 1) * 128], kz.ap(j * 128, j * 128 + 128), vr.ap(j * 128, j * 128 + 128),
                         True, True, R=kz.T() + vr.T(), W=[ps.T(bks[j // 4])])
                for j in order:
                    K.cp(Sall.ap(j * 128, j * 128 + 128), S, R=ST, W=Sall.T(), eng="act")
                    hs.add(j)
                    K.ts(S, S, gg, None, ALU.mult, None, R=ST + DT, W=ST)
                    K.tt(S, S, ps[:, bks[j // 4], (j % 4) * 128:(j % 4 + 1) * 128], ALU.add,
                         R=ST + [ps.T(bks[j // 4])], W=ST)

        def sample_halo_prep(l):
            K.dma("sp", zh[:, 0:32].rearrange("p (r c) -> p r c", r=4), ccv_ap(5120, 5128), R=ccdT, W=[zh.T()])
            zgv = zh[:, 0:32].rearrange("p (r c s) -> p r c s", r=4, c=4)
            for selbase, s_idx, ocol in ((38, 1, 0), (42, 0, 1)):
                out = zh[:, 32:40].rearrange("p (c s) -> p c s", s=2)[:, :, ocol]
                K.ts(out, zgv[:, 0, :, s_idx], ctab[:, C_COL + selbase:C_COL + selbase + 1], None, ALU.mult, None,
                     R=[zh.T(), ctab.T()], W=[zh.T()])
                for i in range(1, 4):
                    K.stt(out, zgv[:, i, :, s_idx], ctab[:, C_COL + selbase + i:C_COL + selbase + i + 1], out, ALU.mult, ALU.add,
                          R=[zh.T(), ctab.T()], W=[zh.T()])
            for tile in range(2):
                K.dma("pool", cvb[:, tile, :].rearrange("p (h d) -> p h d", h=8),
                      cv_d[l][:, tile * 128:(tile + 1) * 128, :].rearrange("h t d -> t h d"), R=[], W=[cvb.T()])
                K.dma("sp", stg[:, tile, 0:512].rearrange("p (h d) -> p h d", h=8),
                      ck_d[l][:, tile * 128:(tile + 1) * 128, :].rearrange("h t d -> t h d"), R=[], W=[stg.T(tile)])
                b = bank()
                for c in range(4):
                    K.tr(ps[:, b, c * 128:(c + 1) * 128], stg[:, tile, c * 128:(c + 1) * 128], identf[:],
                         R=[stg.T(tile), identf.T()], W=[ps.T(b)])
                K.cp(ckT[:, :, tile * 128:(tile + 1) * 128], ps[:, b, :].rearrange("p (c t) -> p c t", c=4), R=[ps.T(b)], W=[ckT.T()])

        def sample_conv_pads(zbuf, c):
            K.cp(zbuf.ap(0, 1), zh[:, 32 + 2 * c:33 + 2 * c], R=[zh.T()], W=zbuf.T())
            K.cp(zbuf.ap(1025, 1026), zh[:, 33 + 2 * c:34 + 2 * c], R=[zh.T()], W=zbuf.T())

        def sample_na_prep(l, hh, knT, vn):
            for a in range(2):
                for h4 in range(4):
                    src = bass.AP(padd_h, l * 8 * 2176 + (4 * hh + h4) * 2176 + (1 - a) * 128, [[1, 64], [128, 16], [1, 64]])
                    K.dma("pool", Ub[a * 64:(a + 1) * 64, h4 * 1024:(h4 + 1) * 1024].rearrange("p (j c) -> p j c", j=16), src,
                          R=[padd.T()], W=[Ub.T()])
            K.stt(Ub[:].rearrange("p (x c) -> p x c", c=64), Ub[:].rearrange("p (x c) -> p x c", c=64), 1.0 / SC,
                  ctab[:, C_COL + 46:C_COL + 110].unsqueeze(1).broadcast_to([128, 64, 64]), ALU.mult, ALU.add,
                  R=[Ub.T(), ctab.T()], W=[Ub.T()])
            stgv = tmpf[:, 0:2, :].rearrange("p a (b c) -> p (a b) c", b=2)
            acc = tmpf[:, 2, 0:256]
            TT = [tmpf.T(0), tmpf.T(1)]

            def combine(col, selbase, out_ap, outW):
                K.dma("sp", stgv, ccv_ap(col, col + 256), R=ccdT, W=TT)
                K.ts(acc, stgv[:, 0, :], ctab[:, C_COL + selbase:C_COL + selbase + 1], None, ALU.mult, None, R=TT + [ctab.T()], W=[tmpf.T(2)])
                for i in (1, 2):
                    K.stt(acc, stgv[:, i, :], ctab[:, C_COL + selbase + i:C_COL + selbase + i + 1], acc, ALU.mult, ALU.add,
                          R=TT + [ctab.T(), tmpf.T(2)], W=[tmpf.T(2)])
                K.stt(out_ap, stgv[:, 3, :], ctab[:, C_COL + selbase + 3:C_COL + selbase + 4], acc, ALU.mult, ALU.add,
                      R=TT + [ctab.T(), tmpf.T(2)], W=outW)
            for cc in range(2):
                c = 2 * hh + cc
                combine(1024 + c * 512 + 256, 38, knT.ap(cc * 1536, cc * 1536 + 256), knT.T(cc * 1536, cc * 1536 + 256))
                combine(1024 + c * 512, 42, knT.ap(cc * 1536 + 1280, cc * 1536 + 1536), knT.T(cc * 1536 + 1280, cc * 1536 + 1536))
            for idx, selbase, tile in ((2, 38, 0), (3, 38, 1), (0, 42, 10), (1, 42, 11)):
                combine(3072 + idx * 512 + hh * 256, selbase, vn.ap(tile * 256, tile * 256 + 256), vn.T(tile * 256, tile * 256 + 256))

        def sample_scores(l, hh, m, hl, cc, pr, qap, knT, vn, qnT):
            pstart = m if m <= 6 else 6
            np_ = 6 if m in (0, 7) else 5
            nk_ = np_ * 128
            j0 = 3 if m <= 6 else 1
            kbase = knT.off + cc * 1536 + pstart * 128
            b0, b1 = bank(), bank()
            w1 = nk_ - 512
            qz = AV(22784 + (pr // 64) * 128, 128)
            K.cp(arb[pr:pr + 64, qz.off:qz.off + 128], qap, R=qnT.T(), W=qz.T())
            qfull = arb[:, qz.off:qz.off + 128]
            for (b, lo, hi) in ((b0, 0, 512), (b1, 512, nk_)):
                wd = hi - lo
                K.mm(ps[:, b, 0:wd], qfull, arb[:, kbase + lo:kbase + hi], True, False, R=qz.T() + knT.T(), W=[ps.T(b)])
                K.mm(ps[:, b, 0:wd], jrev[:], Ub[:, hl * 1024 + j0 * 64 + lo:hl * 1024 + j0 * 64 + hi], False, False,
                     R=[jrev.T(), Ub.T()], W=[ps.T(b)])
                K.mm(ps[:, b, 0:wd], natab[:, m * 128:(m + 1) * 128], natab[:, 1024 + lo:1024 + hi], False, True,
                     R=[natab.T()], W=[ps.T(b)])
            cg = 2 * hh + cc
            K.mm(ps[:, b1, w1:w1 + 256], qfull, ckT[:, cg, :], True, True, R=qz.T() + [ckT.T()], W=[ps.T(b1)])
            segs = [(b0, 0, 512), (b1, 0, w1 + 256)]
            vsrc = [(vn, (pstart + kb) * 256 + hl * 64) for kb in range(np_)]
            vsrc += [(None, cvb[:, tile, (4 * hh + hl) * 64:(4 * hh + hl) * 64 + 64]) for tile in range(2)]
            return segs, vsrc

        DBG = os.environ.get("KDBG", "rcam")
        for l in range(int(os.environ.get("KL", L))):
            layer_vectors(l)
            layer_tables(l)
            ffn(l, 0, 0)
            ffn(l, 0, 1)
            if STAGE == "full":
                mod_norm(1, 1)
                sample_pre(l)
            mixer(l, 0)
            if STAGE == "full" and os.environ.get("KDBGS", "rcam") != "none":
                mixer(l, 1)
            ffn(l, 1, 0)
            ffn(l, 1, 1)

        yTf_ap = ar.t[:, 0:KC * 512].rearrange("p (k t) -> p k t", k=KC)

        class _Y:
            def __getitem__(self, idx):
                return yTf_ap[idx]

            def T(self, kc):
                return ar.T(kc)
        yTf = _Y()
        for g, dst in ((0, yp), (1, ys)):
            for t in range(2):
                norm_to(g, t,
                        lambda kc: vecG[:, 16 + kc:17 + kc],
                        lambda kc: 0.0,
                        lambda kc: yTf[:, kc, :],
                        lambda kc: [yTf.T(kc)])
                for tb in range(4):
                    sl = ld % 2
                    ld += 1
                    for hf in range(2):
                        b = bank()
                        for q in range(4):
                            kc = hf * 4 + q
                            K.tr(ps[:, b, q * 128:(q + 1) * 128], yTf[:, kc, tb * 128:(tb + 1) * 128], identf[:],
                                 R=[yTf.T(kc), identf.T()], W=[ps.T(b)])
                        K.cp(stg[:, sl, hf * 512:(hf + 1) * 512], ps[:, b, :], R=[ps.T(b)], W=[stg.T(sl)],
                             eng="act" if hf else "dve")
                    r0 = t * 512 + tb * 128
                    K.dma("sp", dst[r0:r0 + 128, :], stg[:, sl, :], R=[stg.T(sl)], W=[dst.T(r0)])

        K.finish()
        with nc.Block() as block:
            K.replay(block)
    return nc


_NC_CACHE = {}


def make_ctab(qd):
    f = np.float32
    t = np.zeros((128, NCT), f)
    p = np.arange(128)
    t[:, 0:128] = np.eye(128, dtype=f)
    perm = np.where((p % 64) < 32, p + 32, p - 32)
    sw = np.zeros((128, 128), f)
    sw[perm, p] = 1.0
    t[:, C_SW:C_SW + 128] = sw
    k = p[:, None]
    q = p[None, :]
    t[:, C_RELF:C_RELF + 128] = np.maximum(q - k, 0)
    t[:, C_MF:C_MF + 128] = (q >= k)
    t[:, C_RELB:C_RELB + 128] = np.maximum(k - q, 0)
    t[:, C_MB:C_MB + 128] = (k >= q)
    t[:, C_IO1:C_IO1 + 128] = (p + 1)[None, :]
    t[:, C_IOB:C_IOB + 128] = (128 - p)[None, :]
    c = C_COL
    t[:, c + 0] = 127 - p
    t[:, c + 1] = p
    for j in range(8):
        t[:, c + 2 + j] = 1023 - 128 * j - p
        t[:, c + 10 + j] = 128 * j + p
    t[:, c + 18] = 255 - p
    t[:, c + 19] = 128
    for i in range(4):
        t[:, c + 20 + i] = 1024 * max(qd - 1 - i, 0)
        t[:, c + 24 + i] = float(i < qd)
        t[:, c + 28 + i] = 1024 * max(i - qd - 1, 0)
        t[:, c + 32 + i] = float(i > qd)
        t[:, c + 38 + i] = float(i == qd - 1)
        t[:, c + 42 + i] = float(i == qd + 1)
    t[:, c + 36] = 1024 * qd
    t[:, c + 37] = 1024 * (3 - qd)
    cq = np.arange(64)
    c0 = np.clip(cq - 8, 0, 48)
    ck = np.arange(64)
    ok = (ck[None, :] >= c0[:, None]) & (ck[None, :] < c0[:, None] + 16)
    cm = np.where(ok, 0.0, NEG).astype(f)
    t[0:64, c + 46:c + 110] = cm[::-1]
    t[64:128, c + 46:c + 110] = cm[::-1]
    return t


def kernel(**inp):
    f = np.float32
    nc = _NC_CACHE.get("nc")
    if nc is None:
        nc = _NC_CACHE["nc"] = build_program()
    ident = np.eye(128, dtype=f)
    vecb = np.ascontiguousarray(np.concatenate([inp["b_merge"].reshape(L, 24, 128), inp["conv_w"].reshape(L, 12, 128),
                                                inp["conv_b"].reshape(L, 4, 128)], 1).astype(f))
    dlog = np.ascontiguousarray(np.broadcast_to(inp["ret_decay_logit"].reshape(L, 1, 8), (L, 128, 8)).astype(f))
    in_maps = []
    for c in range(8):
        b, qd = c // 4, c % 4
        m = {
            "xp": np.ascontiguousarray(inp["x_prompt"][4 * c:4 * c + 4].reshape(NT, D)),
            "xs": np.ascontiguousarray(inp["x_sample"][b, qd * NT:(qd + 1) * NT]),
            "vecg": np.ascontiguousarray(np.concatenate([inp["c_ctx"].reshape(8, 128), inp["c"][b].reshape(8, 128),
                                                         inp["final_g"].reshape(8, 128)], 0).astype(f)),
            "ident": ident,
            "norm_g": np.ascontiguousarray(inp["norm_g"].reshape(L, 24, 128)),
            "w_mod": inp["w_mod"],
            "b_mod": np.ascontiguousarray(inp["b_mod"].reshape(L, 72, 128)),
            "ffn_w1": inp["ffn_w1"],
            "ffn_w2": inp["ffn_w2"],
            "w_in": inp["w_in"],
            "w_branch": inp["w_branch"],
            "w_merge": inp["w_merge"],
            "w_out": inp["w_out"],
            "vecb": vecb,
            "dlog": dlog,
            "ctab_in": make_ctab(qd),
        }
        m.update(sample_inputs(inp, b, qd))
        in_maps.append(m)
    res = run_bass_kernel_spmd(nc, in_maps, core_ids=list(range(8)))
    r = res.results
    y_prompt = np.concatenate([r[c]["yp"].reshape(4, 256, D) for c in range(8)], 0)
    y_sample = np.stack([np.concatenate([r[4 * b + q]["ys"] for q in range(4)], 0) for b in range(2)], 0)
    nst = np.concatenate([r[c]["nst"].reshape(4, L, 2, 4, 128, 128) for c in range(8)], 0)
    nk = np.concatenate([r[c]["nk"].reshape(4, L, 8, 256, 64) for c in range(8)], 0)
    nv = np.concatenate([r[c]["nv"].reshape(4, L, 8, 256, 64) for c in range(8)], 0)
    return y_prompt, y_sample, nst, nk, nv


def sample_inputs(inp, b, qd):
    f = np.float32
    p = np.arange(128)
    t = np.arange(1024) + qd * 1024
    i = (p % 64) % 32
    inv = (np.float32(10000.0) ** (-(i.astype(f)) * f(2.0) / f(64.0))).astype(f)
    pos = np.where(p[:, None] < 64, (t // 64)[None, :], (t % 64)[None, :]).astype(f)
    ang = (pos * inv[:, None]).astype(f)
    sign = np.where((p % 64) < 32, -1.0, 1.0).astype(f)[:, None]
    ropet = np.concatenate([np.cos(ang), np.sin(ang) * sign], 1).astype(f)
    natab = np.zeros((128, 1792), f)
    for m in range(8):
        pstart = m if m <= 6 else 6
        for a in range(2):
            i_ = 2 * m + a
            r = 16 * qd + i_
            r0 = min(max(r - 4, 0), 56)
            for s_ in range(12):
                grow = 16 * qd - 4 + 2 * pstart + s_
                ok = (r0 <= grow < r0 + 8)
                natab[s_, m * 128 + a * 64:m * 128 + a * 64 + 64] = 0.0 if ok else NEG
    for s_ in range(12):
        natab[s_, 1024 + s_ * 64:1024 + (s_ + 1) * 64] = 1.0
    jrev = np.zeros((128, 128), f)
    for a in range(2):
        for cq in range(64):
            jrev[a * 64 + 63 - cq, a * 64 + cq] = 1.0
    padinit = np.zeros((L, 8, 17, 128), f)
    padinit[:, :, 0, :] = NEG
    padinit[:, :, 16, :] = NEG
    return {
        "st0": np.ascontiguousarray(inp["state_ret"][b]),
        "ck": np.ascontiguousarray(inp["cache_na_k"][b]),
        "cv": np.ascontiguousarray(inp["cache_na_v"][b]),
        "ropet_in": ropet,
        "natab_in": natab,
        "rpb": inp["na_rpb"],
        "jrev_in": jrev,
        "padinit": padinit.reshape(L, 8, 2176),
    }
```
